# Optimizing a Trainium2 kernel written in Bass

```python
import jax, jax.numpy as jnp
from jax import lax
import numpy as np

D_MODEL = 1024
BATCH = 8
SEQ = 2048
DEPTH = 4
DEC_BATCH = 128
DEC_SEQ = 8
PAST_LEN = 8192
PAGE_SIZE = 128

N_HEADS = 8
HEAD_DIM = 64
N_KV_HEADS = 2
GROUP = N_HEADS // N_KV_HEADS
ATTN_DIM = N_HEADS * HEAD_DIM
KV_DIM = N_KV_HEADS * HEAD_DIM
WINDOW = 128
Q_BLOCK = 128
CONV_DIM = D_MODEL // 4
CONV_K = 31
POOL_DIM = D_MODEL // 4
POOL_WINDOWS = (2, 4, 8, 16)
N_POOL_GROUPS = 4
POOL_GROUP_DIM = POOL_DIM // N_POOL_GROUPS
POOL_MAX = 16
IN_DIM = ATTN_DIM + 2 * KV_DIM + 2 * CONV_DIM + POOL_DIM
MIX_DIM = ATTN_DIM + CONV_DIM + POOL_DIM
D_FF = ((8 * D_MODEL // 3 + 127) // 128) * 128
RMS_EPS = 1e-6
LN_EPS = 1e-5
NEG_INF = -1e30

kernel_name = "hymba_style_swa_conv_pool_macaron_decoder_step"


def rms_norm(x, g):
    xf = x.astype(jnp.float32)
    y = xf * lax.rsqrt(jnp.mean(xf * xf, axis=-1, keepdims=True) + RMS_EPS)
    return (y * g.astype(jnp.float32)).astype(x.dtype)


def layer_norm(x, g, b):
    xf = x.astype(jnp.float32)
    mu = jnp.mean(xf, axis=-1, keepdims=True)
    xc = xf - mu
    y = xc * lax.rsqrt(jnp.mean(xc * xc, axis=-1, keepdims=True) + LN_EPS)
    return (y * g.astype(jnp.float32) + b.astype(jnp.float32)).astype(x.dtype)


def swiglu(x, wg, wu, wd):
    return (jax.nn.silu(x @ wg) * (x @ wu)) @ wd


def alibi_slopes():
    return jnp.exp2(-8.0 * jnp.arange(1, N_HEADS + 1, dtype=jnp.float32) / N_HEADS)


def window_attention(q, k_ext, v_ext, sinks, pos0):
    B, T = q.shape[0], q.shape[1]
    bq = T if T <= Q_BLOCK else Q_BLOCK
    nb = T // bq
    span = WINDOW + bq
    kidx = jnp.arange(nb)[:, None] * bq + jnp.arange(span)[None, :]
    kb = k_ext[:, kidx]
    vb = v_ext[:, kidx]
    qb = q.reshape(B, nb, bq, N_KV_HEADS, GROUP, HEAD_DIM)
    scale = HEAD_DIM ** -0.5
    logits = jnp.einsum('bnqkgd,bnskd->bnkgqs', qb.astype(jnp.float32),
                        kb.astype(jnp.float32)) * scale
    t_pos = pos0 + jnp.arange(nb)[:, None] * bq + jnp.arange(bq)[None, :]
    s_pos = pos0 - WINDOW + kidx
    dist = t_pos[:, :, None] - s_pos[:, None, :]
    valid = (dist >= 0) & (dist < WINDOW) & (s_pos[:, None, :] >= 0)
    slopes = alibi_slopes().reshape(N_KV_HEADS, GROUP)
    bias = -slopes[None, :, :, None, None] * dist[:, None, None, :, :].astype(jnp.float32)
    logits = jnp.where(valid[:, None, None], logits + bias, NEG_INF)
    sink = sinks.astype(jnp.float32).reshape(N_KV_HEADS, GROUP)[None, None, :, :, None, None]
    m = jnp.maximum(jnp.max(logits, axis=-1, keepdims=True), sink)
    e = jnp.exp(logits - m)
    p = e / (jnp.sum(e, axis=-1, keepdims=True) + jnp.exp(sink - m))
    out = jnp.einsum('bnkgqs,bnskd->bnqkgd', p.astype(v_ext.dtype), vb)
    return out.reshape(B, T, ATTN_DIM)


def conv_module(g_ext, dw_w, dw_b, ln_g, ln_b, pw_w):
    c = lax.conv_general_dilated(g_ext, dw_w[:, None, :].astype(g_ext.dtype), (1,), 'VALID',
                                 dimension_numbers=('NWC', 'WIO', 'NWC'),
                                 feature_group_count=CONV_DIM) + dw_b
    c = layer_norm(c, ln_g, ln_b)
    return jax.nn.silu(c) @ pw_w


def pool_mixer(p_ext, pos0, pool_w, pool_scale):
    P = POOL_MAX - 1
    B, T = p_ext.shape[0], p_ext.shape[1] - P
    pf = p_ext.astype(jnp.float32)
    csum = jnp.cumsum(jnp.pad(pf, ((0, 0), (1, 0), (0, 0))), axis=1)
    cur = pf[:, P:]
    pos = pos0 + jnp.arange(T)
    diffs = []
    for g, w in enumerate(POOL_WINDOWS):
        sl = slice(g * POOL_GROUP_DIM, (g + 1) * POOL_GROUP_DIM)
        wsum = csum[:, P + 1:P + 1 + T, sl] - csum[:, P + 1 - w:P + 1 - w + T, sl]
        cnt = jnp.minimum(w, pos + 1).astype(jnp.float32)[None, :, None]
        diffs.append(wsum / cnt - cur[:, :, sl])
    d = jnp.stack(diffs, axis=2)
    out = jnp.einsum('btgc,gcd->btgd', d, pool_w.astype(jnp.float32)).reshape(B, T, POOL_DIM)
    return (out * pool_scale.astype(jnp.float32)).astype(p_ext.dtype)


def trunk_layer(x, k_buf, v_buf, conv_buf, pool_buf, pos0,
                norm_g, f1g, f1u, f1d, w_in, sinks, dw_w, dw_b, ln_g, ln_b, pw_w,
                pool_w, pool_scale, w_out, f2g, f2u, f2d):
    B, T = x.shape[0], x.shape[1]
    h = x + 0.5 * rms_norm(swiglu(rms_norm(x, norm_g[0]), f1g, f1u, f1d), norm_g[1])
    u = rms_norm(h, norm_g[2])
    z = u @ w_in
    o = 0
    q = z[..., o:o + ATTN_DIM].reshape(B, T, N_HEADS, HEAD_DIM); o += ATTN_DIM
    k = z[..., o:o + KV_DIM].reshape(B, T, N_KV_HEADS, HEAD_DIM); o += KV_DIM
    v = z[..., o:o + KV_DIM].reshape(B, T, N_KV_HEADS, HEAD_DIM); o += KV_DIM
    ga = z[..., o:o + CONV_DIM]; o += CONV_DIM
    gb = z[..., o:o + CONV_DIM]; o += CONV_DIM
    pu = z[..., o:o + POOL_DIM]
    k_ext = jnp.concatenate([k_buf, k], axis=1)
    v_ext = jnp.concatenate([v_buf, v], axis=1)
    attn = window_attention(q, k_ext, v_ext, sinks, pos0)
    g_ext = jnp.concatenate([conv_buf, ga * jax.nn.sigmoid(gb)], axis=1)
    conv = conv_module(g_ext, dw_w, dw_b, ln_g, ln_b, pw_w)
    p_ext = jnp.concatenate([pool_buf, pu], axis=1)
    pool = pool_mixer(p_ext, pos0, pool_w, pool_scale)
    mix = jnp.concatenate([attn, conv, pool], axis=-1) @ w_out
    h = h + rms_norm(mix, norm_g[3])
    y = h + 0.5 * rms_norm(swiglu(rms_norm(h, norm_g[4]), f2g, f2u, f2d), norm_g[5])
    return (y, k_ext[:, -WINDOW:], v_ext[:, -WINDOW:],
            g_ext[:, -(CONV_K - 1):], p_ext[:, -(POOL_MAX - 1):])


def setup_inputs(seed: int = 0) -> dict:
    key = jax.random.key(seed)
    ks = jax.random.split(key, 24)
    nrm = lambda k, s, sc: jax.random.normal(k, s, jnp.float32) * sc
    return {
        "x_prompt": nrm(ks[0], (BATCH, SEQ, D_MODEL), 1.0),
        "x_sample": nrm(ks[1], (DEC_BATCH, DEC_SEQ, D_MODEL), 1.0),
        "state_attn_k": nrm(ks[2], (DEPTH, DEC_BATCH, WINDOW, N_KV_HEADS, HEAD_DIM), 1.0),
        "state_attn_v": nrm(ks[3], (DEPTH, DEC_BATCH, WINDOW, N_KV_HEADS, HEAD_DIM), 1.0),
        "state_conv": nrm(ks[4], (DEPTH, DEC_BATCH, CONV_K - 1, CONV_DIM), 0.5),
        "state_pool": nrm(ks[5], (DEPTH, DEC_BATCH, POOL_MAX - 1, POOL_DIM), 1.0),
        "norm_g": 1.0 + nrm(ks[6], (DEPTH, 6, D_MODEL), 0.02),
        "ffn1_wg": nrm(ks[7], (DEPTH, D_MODEL, D_FF), D_MODEL ** -0.5),
        "ffn1_wu": nrm(ks[8], (DEPTH, D_MODEL, D_FF), D_MODEL ** -0.5),
        "ffn1_wd": nrm(ks[9], (DEPTH, D_FF, D_MODEL), D_FF ** -0.5),
        "w_in": nrm(ks[10], (DEPTH, D_MODEL, IN_DIM), D_MODEL ** -0.5),
        "attn_sinks": nrm(ks[11], (DEPTH, N_HEADS), 0.5),
        "conv_dw_w": nrm(ks[12], (DEPTH, CONV_K, CONV_DIM), CONV_K ** -0.5),
        "conv_dw_b": nrm(ks[13], (DEPTH, CONV_DIM), 0.02),
        "conv_ln_g": 1.0 + nrm(ks[14], (DEPTH, CONV_DIM), 0.02),
        "conv_ln_b": nrm(ks[15], (DEPTH, CONV_DIM), 0.02),
        "conv_pw_w": nrm(ks[16], (DEPTH, CONV_DIM, CONV_DIM), CONV_DIM ** -0.5),
        "pool_w": nrm(ks[17], (DEPTH, N_POOL_GROUPS, POOL_GROUP_DIM, POOL_GROUP_DIM), POOL_GROUP_DIM ** -0.5),
        "pool_scale": 1.0 + nrm(ks[18], (DEPTH, POOL_DIM), 0.1),
        "w_out": nrm(ks[19], (DEPTH, MIX_DIM, D_MODEL), MIX_DIM ** -0.5),
        "ffn2_wg": nrm(ks[20], (DEPTH, D_MODEL, D_FF), D_MODEL ** -0.5),
        "ffn2_wu": nrm(ks[21], (DEPTH, D_MODEL, D_FF), D_MODEL ** -0.5),
        "ffn2_wd": nrm(ks[22], (DEPTH, D_FF, D_MODEL), D_FF ** -0.5),
    }


def reference(x_prompt, x_sample, state_attn_k, state_attn_v, state_conv, state_pool,
              norm_g, ffn1_wg, ffn1_wu, ffn1_wd, w_in, attn_sinks, conv_dw_w, conv_dw_b,
              conv_ln_g, conv_ln_b, conv_pw_w, pool_w, pool_scale, w_out,
              ffn2_wg, ffn2_wu, ffn2_wd):
    params = (norm_g, ffn1_wg, ffn1_wu, ffn1_wd, w_in, attn_sinks, conv_dw_w, conv_dw_b,
              conv_ln_g, conv_ln_b, conv_pw_w, pool_w, pool_scale, w_out,
              ffn2_wg, ffn2_wu, ffn2_wd)
    dt = x_prompt.dtype
    zk = jnp.zeros((BATCH, WINDOW, N_KV_HEADS, HEAD_DIM), dt)
    zc = jnp.zeros((BATCH, CONV_K - 1, CONV_DIM), dt)
    zp = jnp.zeros((BATCH, POOL_MAX - 1, POOL_DIM), dt)
    hp, hs = x_prompt, x_sample
    kp, vp, cp, pp, ksl, vsl, csl, psl = [], [], [], [], [], [], [], []
    for l in range(DEPTH):
        lp = [p[l] for p in params]
        hp, k1, v1, c1, p1 = trunk_layer(hp, zk, zk, zc, zp, 0, *lp)
        hs, k2, v2, c2, p2 = trunk_layer(hs, state_attn_k[l], state_attn_v[l],
                                         state_conv[l], state_pool[l], PAST_LEN, *lp)
        kp.append(k1); vp.append(v1); cp.append(c1); pp.append(p1)
        ksl.append(k2); vsl.append(v2); csl.append(c2); psl.append(p2)
    return (hp, hs,
            jnp.stack(kp), jnp.stack(vp), jnp.stack(cp), jnp.stack(pp),
            jnp.stack(ksl), jnp.stack(vsl), jnp.stack(csl), jnp.stack(psl))
```

```python
import numpy as np
from contextlib import ExitStack
import concourse.bass as bass
import concourse.mybir as mybir
from concourse.bass_utils import run_bass_kernel_spmd
import ml_dtypes

F32 = mybir.dt.float32
BF16 = mybir.dt.bfloat16
AF = mybir.ActivationFunctionType
ALU = mybir.AluOpType

P = 128
D = 1024
KC = 8
DFF = 2816
FC = 22
NPR = 2048
NS = 128
NT = NPR + NS
GROUPS = [(0, 512), (512, 512), (1024, 512), (1536, 512), (2048, 128)]
RMS_EPS = 1e-6
LN_EPS = 1e-5
NEG = -1e30
NCORES = 8
FN = 544
FBLK = [(0, 272), (272, 272)]
NFG = 4


class SemO:
    def __init__(self, handle):
        self.handle = handle
        self.count = 0


class Reg:
    def __init__(self, name, pending=None):
        self.name = name
        self.lw = None
        self.rd = dict(pending) if pending else {}


class Eng:
    def __init__(self, name, sem):
        self.name = name
        self.sem = sem
        self.ops = []
        self.waited = {}


class Arena:
    def __init__(self):
        self.subs = []

    def recarve(self):
        pend = {}
        for r in self.subs:
            if r.lw is not None:
                s, v = r.lw
                pend[s] = max(pend.get(s, 0), v)
            for s, v in r.rd.items():
                pend[s] = max(pend.get(s, 0), v)
        self.subs = []
        self.pend = pend

    def reg(self, name):
        r = Reg(name, self.pend)
        self.subs.append(r)
        return r


class Trk:
    def __init__(self, nc, es):
        self.nc = nc
        self.es = es
        self.engs = {}
        for n in ("pe", "act", "dve", "pool", "sp"):
            self.engs[n] = Eng(n, self.newsem("e_" + n))

    def newsem(self, name):
        return SemO(self.es.enter_context(self.nc.semaphore(name)))

    def _waits(self, E, rd, wr, nowaw=False):
        need = {}

        def add(tok):
            if tok is None:
                return
            s, v = tok
            if need.get(s, 0) < v:
                need[s] = v

        for r in rd:
            add(r.lw)
            if getattr(r, "psum", False):
                for s_, v_ in r.rd.items():
                    if s_ is not E.sem:
                        add((s_, v_))
        for w in wr:
            if not (nowaw and w.lw is not None and w.lw[0] is E.sem):
                add(w.lw)
            for s, v in w.rd.items():
                add((s, v))
        for s, v in need.items():
            if s is E.sem and E.name == "pe":
                continue
            if E.waited.get(s, 0) >= v:
                continue
            E.waited[s] = v
            E.ops.append(("w", s, v))

    def op(self, en, fn, rd=(), wr=(), mark=True, nowaw=False):
        E = self.engs[en]
        self._waits(E, rd, wr, nowaw)
        if mark:
            E.sem.count += 1
            tok = (E.sem, E.sem.count)
        else:
            tok = (E.sem, E.sem.count + 1)
        E.ops.append(("o", fn, mark))
        for r in rd:
            if r.rd.get(E.sem, 0) < tok[1]:
                r.rd[E.sem] = tok[1]
        for w in wr:
            w.lw = tok
            w.rd = {}
        return tok

    def dma(self, qn, out, in_, rd, wr, dsem):
        Q = self.engs[qn]
        self._waits(Q, rd, wr)
        dsem.count += 16
        tok = (dsem, dsem.count)
        Q.ops.append(("d", out, in_, dsem))
        for r in rd:
            if r.rd.get(dsem, 0) < tok[1]:
                r.rd[dsem] = tok[1]
        for w in wr:
            w.lw = tok
            w.rd = {}
        return tok

    def wait_tok(self, en, tok):
        E = self.engs[en]
        s, v = tok
        if E.waited.get(s, 0) < v:
            E.waited[s] = v
            E.ops.append(("w", s, v))

    def replay(self, block):
        nc = self.nc

        def run(E, e):
            for o in E.ops:
                if o[0] == "w":
                    e.wait_ge(o[1].handle, o[2])
                elif o[0] == "o":
                    ins = o[1](e)
                    if o[2]:
                        ins.then_inc(E.sem.handle, 1)
                else:
                    e.dma_start(out=o[1], in_=o[2]).then_inc(o[3].handle, 16)

        engs = self.engs

        @block.tensor
        def _(e):
            run(engs["pe"], e)

        @block.scalar
        def _(e):
            run(engs["act"], e)

        @block.vector
        def _(e):
            run(engs["dve"], e)

        @block.gpsimd
        def _(e):
            run(engs["pool"], e)

        @block.sync
        def _(e):
            run(engs["sp"], e)


def _slopes():
    return np.exp2(-8.0 * np.arange(1, 9, dtype=np.float64) / 8.0)


def make_consts():
    sl = _slopes()
    j = np.arange(128)[:, None]
    i = np.arange(128)[None, :]
    maskp = np.zeros((128, 2, 2, 2, 2, 128), np.float64)
    for c in range(4):
        for hi in range(2):
            h = c + 4 * hi
            dprev = i + 128 - j
            maskp[:, c // 2, hi, c % 2, 0, :] = np.where(j > i, -sl[h] * dprev, NEG)
            dcur = i - j
            maskp[:, c // 2, hi, c % 2, 1, :] = np.where(j <= i, -sl[h] * dcur, NEG)
    msp = np.zeros((128, 2, 16, 4, 8), np.float64)
    t = np.arange(8)[None, :]
    for kvh in range(2):
        for g in range(4):
            h = kvh * 4 + g
            dist = t + 128 - j
            msp[:, kvh, :, g, :] = np.where(j > t, -sl[h] * dist, NEG)[:, None, :]
    msc = np.full((128, 2, 4, 128), NEG, np.float64)
    sp_ = np.arange(128) // 8
    tp_ = np.arange(128) % 8
    same = sp_[:, None] == sp_[None, :]
    dt_ = tp_[None, :] - tp_[:, None]
    ok = same & (dt_ >= 0)
    for c in range(4):
        for hi in range(2):
            h = c + 4 * hi
            msc[:, hi, c, :] = np.where(ok, -sl[h] * dt_, NEG)
    bf = ml_dtypes.bfloat16
    wins = (2, 4, 8, 16)
    invw = np.zeros((128, 2), np.float32)
    invcnt = np.zeros((128, 2, 16), np.float32)
    for pc in range(2):
        for half in range(2):
            w = wins[pc * 2 + half]
            invw[half * 64:(half + 1) * 64, pc] = 1.0 / w
            cnt = np.minimum(w, np.arange(16) + 1).astype(np.float32)
            invcnt[half * 64:(half + 1) * 64, pc, :] = 1.0 / cnt
    return {
        "c_ident": np.eye(128, dtype=np.float32),
        "c_maskp": maskp.reshape(128, 2048).astype(np.float32).astype(bf),
        "c_msp": msp.reshape(128, 1024).astype(np.float32).astype(bf),
        "c_msc": msc.reshape(128, 1024).astype(np.float32).astype(bf),
        "c_invw": invw,
        "c_invcnt": invcnt.reshape(128, 32),
    }


def build_program(DEPTH, PH=('f1', 'mx', 'f2'), CP=True, FL=4, NGRP=5, DBG=(), ML=7):
    nc = bass.Bass("TRN2", target_bir_lowering=False)
    L = DEPTH

    def din(name, shape, dt=F32):
        return nc.dram_tensor(name, list(shape), dt, kind="ExternalInput").ap()

    def dout(name, shape):
        return nc.dram_tensor(name, list(shape), F32, kind="ExternalOutput").ap()

    x_d = din("x", [NT, D])
    sk_d = din("sk", [L, 16, 128, 128])
    sv_d = din("sv", [L, 16, 128, 128])
    sc_d = din("sc", [L, 16, 30, 256])
    sp_d = din("sp", [L, 16, 15, 256])
    ng_d = din("ng", [L * 48, 128])
    wts = {}
    for nm, shp in (("f1g", [L, D, DFF]), ("f1u", [L, D, DFF]), ("f1d", [L, DFF, D]),
                    ("win", [L, D, 1536]), ("wout", [L, D, D]),
                    ("f2g", [L, D, DFF]), ("f2u", [L, D, DFF]), ("f2d", [L, DFF, D])):
        wts[nm] = din(nm, shp)
    sinks_d = din("sinks", [1, L * 8])
    dww_d = din("dww", [L * 31, 256])
    vecs_d = din("vecs", [L * 4, 256])
    pww_d = din("pww", [L, 256, 256])
    poolw_d = din("poolw", [L, 256, 64])
    ident_d = din("c_ident", [128, 128])
    maskp_d = din("c_maskp", [128, 2048], BF16)
    msp_d = din("c_msp", [128, 1024], BF16)
    msc_d = din("c_msc", [128, 1024], BF16)
    invw_d = din("c_invw", [128, 2])
    invcnt_d = din("c_invcnt", [128, 32])

    y_d = dout("y", [NT, D])
    kp_d = dout("kp", [L, 128, 128])
    vp_d = dout("vp", [L, 128, 128])
    cp_d = dout("cp", [L, 30, 256])
    pp_d = dout("pp", [L, 15, 256])
    ks_d = dout("ks", [L, 16, 128, 128])
    vs_d = dout("vs", [L, 16, 128, 128])
    cs_d = dout("cs", [L, 16, 30, 256])
    ps_d = dout("ps", [L, 16, 15, 256])

    es = ExitStack()
    with es:
        T = Trk(nc, es)
        op, dma = T.op, T.dma

        def sb(name, shape, dt):
            return nc.alloc_sbuf_tensor(name, list(shape), dt)


        xT = sb("xT", [P, KC, NT], F32)
        big = sb("big", [P, FC * FN], BF16)
        obuf = sb("obuf", [P, KC * FN], F32)
        xnT = sb("xnT", [P, KC, FN], BF16)
        sq = sb("sq", [P, KC, FN], BF16)
        NS4 = 4
        s4 = [sb(f"s4_{i}", [P, KC, 256], BF16) for i in range(NS4)]
        s11 = [sb(f"s11_{i}", [P, FC * 256], BF16) for i in range(2)]
        rstd = [sb(f"rstd{i}", [P, FN], F32) for i in range(2)]
        sgb = [sb(f"sg{i}", [P, FN], F32) for i in range(2)]
        tmpb = [sb(f"tmp{i}", [P, FN], F32) for i in range(2)]
        dring = [sb(f"dg{i}", [P, 128], BF16) for i in range(8)]
        PTb = [sb(f"PT{i}", [P, 512], BF16) for i in range(4)]
        Rb = [sb(f"R{i}", [P, 128], F32) for i in range(2)]
        stgo = sb("stgo", [P, 768], F32)
        puc = sb("puc", [P, 2, 16], F32)
        ident = sb("ident", [P, 128], F32)
        identb = sb("identb", [P, 128], BF16)
        onesb = sb("onesb", [P, 128], BF16)
        onesm = sb("onesm", [P, 128], BF16)
        ones256 = sb("ones256", [P, 128], BF16)
        ones32 = sb("ones32", [P, 128], F32)
        maskp = sb("maskp", [P, 2048], BF16)
        msp = sb("msp", [P, 1024], BF16)
        msc = sb("msc", [P, 1024], BF16)
        invw = sb("invw", [P, 2], F32)
        invcnt = sb("invcnt", [P, 32], F32)
        gT = sb("gT", [P, L * 48], F32)
        gTh = sb("gTh", [P, L * 48], F32)
        dwT = sb("dwT", [P, 2, L * 31], F32)
        vT = sb("vT", [P, 2, L * 4], F32)
        esk = sb("esk", [P, L * 8], F32)
        esc = sb("esc", [P, L * 4], F32)
        pwb = sb("pwb", [P, L, 2, 256], BF16)
        poolwb = sb("poolwb", [P, L, 2, 128], BF16)
        sinks_sb = sb("sinks_sb", [1, L * 8], F32)
        epsr = sb("epsr", [P, 1], F32)
        negt = sb("negt", [P, 128], BF16)
        epsl = sb("epsl", [P, 1], F32)

        pbank = [nc.alloc_psum_tensor(f"pb{i}", [P, 512], F32) for i in range(8)]
        pregs = [Reg(f"pb{i}") for i in range(8)]
        for r_ in pregs:
            r_.psum = True
        pctr = [0]

        def palloc():
            i = pctr[0] % 8
            pctr[0] += 1
            return pbank[i], pregs[i]

        A_big, A_obuf, A_xn = Arena(), Arena(), Arena()
        A_s11 = [Arena() for _ in range(4)]
        for a in [A_big, A_obuf, A_xn] + A_s11:
            a.recarve()

        def multi_reg(name, arenas):
            pend = {}
            for a in arenas:
                for s_, v_ in a.pend.items():
                    pend[s_] = max(pend.get(s_, 0), v_)
            r = Reg(name, pend)
            for a in arenas:
                a.subs.append(r)
            return r

        NG = NGRP
        r_xT = [Reg(f"xT{g}") for g in range(4)]
        r_xTs = [Reg(f"xTs{g}") for g in range(4)]

        def xregs(gi_):
            return [r_xT[gi_]] if gi_ < 4 else list(r_xTs)
        r_sq = Reg("sq")
        r_rstd = [Reg("rstd0"), Reg("rstd1")]
        r_sg = [Reg("sg0"), Reg("sg1")]
        r_tmp = [Reg("tmp0"), Reg("tmp1")]
        r_PT = [Reg(f"PT{i}") for i in range(4)]
        r_R = [Reg("R0"), Reg("R1")]
        r_const = Reg("const")
        r_puc = Reg("puc")
        r_dring = [Reg(f"dg{i}") for i in range(8)]
        r_stgo = [Reg("stgo0"), Reg("stgo1"), Reg("stgo2")]
        ctr = {"rstd": 0, "sg": 0, "tmp": 0, "PT": 0, "R": 0, "dg": 0}

        def rot(kind, bufs, regs):
            i = ctr[kind] % len(bufs)
            ctr[kind] += 1
            return bufs[i], regs[i]

        d_c1 = T.newsem("d_c1")
        d_c2 = T.newsem("d_c2")
        d_ld = T.newsem("d_ld")
        d_stgo = [T.newsem(f"d_stgo{i}") for i in range(3)]
        d_y = [T.newsem(f"d_y{i}") for i in range(2)]
        d_s4 = [T.newsem(f"d_s4_{i}") for i in range(NS4)]
        d_s11 = [T.newsem(f"d_s11_{i}") for i in range(4)]
        d_xin = [T.newsem(f"d_xin{i}") for i in range(2)]
        d_stK = [T.newsem("d_stK0"), T.newsem("d_stK1")]
        d_stC = T.newsem("d_stC")
        d_vs = T.newsem("d_vs")
        d_cp = T.newsem("d_cp")
        r_s4 = [Reg(f"s4_{i}") for i in range(NS4)]

        s4_list, s11_list = [], []
        for l in range(L):
            for gi in range(NFG if 'f1' in PH else 0):
                for fp in range(11):
                    s4_list.append(wts["f1g"][l, :, fp * 256:(fp + 1) * 256])
                    s4_list.append(wts["f1u"][l, :, fp * 256:(fp + 1) * 256])
                for dp in range(4):
                    for hh in range(2):
                        s11_list.append(wts["f1d"][l, hh * 1408:(hh + 1) * 1408, dp * 256:(dp + 1) * 256])
            for gi in range(NG if 'mx' in PH else 0):
                for pi in range(6):
                    s4_list.append(wts["win"][l, :, pi * 256:(pi + 1) * 256])
                for pi in range(4):
                    s4_list.append(wts["wout"][l, :, pi * 256:(pi + 1) * 256])
            for gi in range(NFG if 'f2' in PH else 0):
                for fp in range(11):
                    s4_list.append(wts["f2g"][l, :, fp * 256:(fp + 1) * 256])
                    s4_list.append(wts["f2u"][l, :, fp * 256:(fp + 1) * 256])
                for dp in range(4):
                    for hh in range(2):
                        s11_list.append(wts["f2d"][l, hh * 1408:(hh + 1) * 1408, dp * 256:(dp + 1) * 256])
        s4_state = {"issued": 0, "next": 0}
        s11_state = {"issued": 0, "next": 0}
        s11_regs = [None] * 4

        def s4_issue():
            j = s4_state["issued"]
            if j >= len(s4_list):
                return
            s = j % NS4
            src = s4_list[j].rearrange("(k p) n -> p k n", p=P)
            dma("pool", s4[s][:, :, :], src, [], [r_s4[s]], d_s4[s])
            s4_state["issued"] = j + 1

        def s4_next():
            j = s4_state["next"]
            assert j < s4_state["issued"]
            s4_state["next"] = j + 1
            s = j % NS4
            return s4[s], r_s4[s]

        def s11_ap(hs):
            t_, h_ = hs // 2, hs % 2
            return s11[t_][:, h_ * 2816:(h_ + 1) * 2816].rearrange("p (k n) -> p k n", k=11)

        def s11_issue():
            j = s11_state["issued"]
            if j >= len(s11_list):
                return
            hs = j % 4
            ar = A_s11[hs]
            ar.recarve()
            r = ar.reg(f"s11w{hs}")
            s11_regs[hs] = r
            src = s11_list[j].rearrange("(k p) n -> p k n", p=P)
            dma("pool", s11_ap(hs), src, [], [r], d_s11[hs])
            s11_state["issued"] = j + 1

        def s11_next():
            j = s11_state["next"]
            assert j < s11_state["issued"], (j, s11_state)
            s11_state["next"] = j + 1
            hs = j % 4
            return s11_ap(hs), s11_regs[hs]

        for dst, src in ((ident[:, :], ident_d[:, :]), (maskp[:, :], maskp_d[:, :]), (msp[:, :], msp_d[:, :]),
                         (msc[:, :], msc_d[:, :]), (invw[:, :], invw_d[:, :]), (invcnt[:, :], invcnt_d[:, :]),
                         (sinks_sb[:, :], sinks_d[:, :])):
            dma("sp", dst, src, [], [r_const], d_c1)
        op("dve", lambda e: e.memset(onesb[:, :], 1.0), [], [r_const])
        op("dve", lambda e: e.memset(onesm[:, :], 1.0 / 1024.0), [], [r_const])
        op("dve", lambda e: e.memset(ones256[:, :], 1.0 / 256.0), [], [r_const])
        op("dve", lambda e: e.memset(ones32[:, :], 1.0), [], [r_const])
        op("dve", lambda e: e.memset(epsr[:, :], RMS_EPS), [], [r_const])
        op("dve", lambda e: e.memset(negt[:, :], NEG), [], [r_const])
        op("dve", lambda e: e.memset(epsl[:, :], LN_EPS), [], [r_const])
        op("dve", lambda e: e.memset(poolwb[:, :, :, :], 0.0), [], [r_const])
        op("dve", lambda e: e.tensor_copy(out=identb[:, :], in_=ident[:, :]), [r_const], [r_const])
        r_const2 = Reg("const2")
        T.wait_tok("pool", r_const.lw)
        for l in range(L):
            dma("pool", pwb[:, l, :, :], pww_d[l].rearrange("(k p) n -> p k n", p=P), [], [r_const2], d_c2)
            for g in range(4):
                pc, half = g // 2, g % 2
                dma("pool", poolwb[half * 64:(half + 1) * 64, l, pc, half * 64:(half + 1) * 64],
                    poolw_d[l, g * 64:(g + 1) * 64, :], [], [r_const2], d_c2)

        for l in range(L if CP else 0):
            dma("sp", ks_d[l, :, 0:120, :], sk_d[l, :, 8:128, :], [], [], d_cp)
            dma("sp", vs_d[l, :, 0:120, :], sv_d[l, :, 8:128, :], [], [], d_cp)
            dma("sp", cs_d[l, :, 0:22, :], sc_d[l, :, 8:30, :], [], [], d_cp)
            dma("sp", ps_d[l, :, 0:7, :], sp_d[l, :, 8:15, :], [], [], d_cp)

        for _ in range(NS4):
            s4_issue()
        for _ in range(4):
            s11_issue()

        r_stg = A_obuf.reg("stg")

        def load_T(src_rows, nrows, nchunks, dst_fn):
            dma("sp", obuf[0:nrows, 0:nchunks * 128], src_rows, [], [r_stg], d_ld)
            for cc in range(nchunks):
                pb, pr = palloc()
                op("pe", lambda e, cc=cc, pb=pb: e.transpose(out=pb[:, 0:nrows], in_=obuf[0:nrows, cc * 128:(cc + 1) * 128],
                                                              identity=ident[0:nrows, 0:nrows]),
                   [r_stg, r_const], [pr])
                op("dve", lambda e, cc=cc, pb=pb: e.tensor_copy(out=dst_fn(cc), in_=pb[:, 0:nrows]), [pr], [r_const])

        nrow_g = L * 48
        r0 = 0
        while r0 < nrow_g:
            nr = min(96, nrow_g - r0)
            load_T(ng_d[r0:r0 + nr, :], nr, 1, lambda cc, r0=r0, nr=nr: gT[:, r0:r0 + nr])
            r0 += nr
        op("dve", lambda e: e.tensor_scalar(out=gTh[:, :], in0=gT[:, :], scalar1=0.5, scalar2=None, op0=ALU.mult),
           [r_const], [r_const])
        load_T(dww_d[:, :], L * 31, 2, lambda cc: dwT[:, cc, :])
        load_T(vecs_d[:, :], L * 4, 2, lambda cc: vT[:, cc, :])
        pb, pr = palloc()
        op("pe", lambda e, pb=pb: e.matmul(pb[:, 0:L * 8], lhsT=ones32[0:1, :], rhs=sinks_sb[0:1, :], start=True, stop=True),
           [r_const], [pr])
        op("act", lambda e, pb=pb: e.activation(out=esk[:, :], in_=pb[:, 0:L * 8], func=AF.Exp), [pr], [r_const])
        eskv = esk[:, :].rearrange("p (l h) -> p l h", h=8)
        op("dve", lambda e: e.tensor_copy(out=esc[0:64, :].rearrange("p (l c) -> p l c", c=4), in_=eskv[0:64, :, 0:4]),
           [r_const], [r_const])
        op("dve", lambda e: e.tensor_copy(out=esc[64:128, :].rearrange("p (l c) -> p l c", c=4), in_=eskv[64:128, :, 4:8]),
           [r_const], [r_const])

        xin = [big[:, i * 2048:(i + 1) * 2048].bitcast(F32) for i in range(2)]
        r_xin = [A_big.reg("xin0"), A_big.reg("xin1")]
        for t in range(NT // P):
            i = t % 2
            gi = min(t // 4, 4)
            dma("sp", xin[i], x_d[t * P:(t + 1) * P, :], [], [r_xin[i]], d_xin[i])
            for hb in range(2):
                pb, pr = palloc()
                for kk in range(4):
                    k = hb * 4 + kk
                    op("pe", lambda e, pb=pb, kk=kk, k=k, i=i: e.transpose(out=pb[:, kk * P:(kk + 1) * P],
                                                                         in_=xin[i][:, k * P:(k + 1) * P], identity=ident[:, :]),
                       [r_xin[i], r_const], [pr], mark=(kk == 3))
                if hb == 0:
                    op("act", lambda e, pb=pb, hb=hb, t=t: e.activation(
                        out=xT[:, hb * 4:hb * 4 + 4, t * P:(t + 1) * P],
                        in_=pb[:, :].rearrange("p (a b) -> p a b", a=4), func=AF.Copy), [pr], xregs(gi))
                else:
                    op("dve", lambda e, pb=pb, hb=hb, t=t: e.tensor_copy(
                        out=xT[:, hb * 4:hb * 4 + 4, t * P:(t + 1) * P],
                        in_=pb[:, :].rearrange("p (a b) -> p a b", a=4)), [pr], xregs(gi))

        def norm_stats(src_reg, n, src=None):
            src = sq if src is None else src
            pb, pr = palloc()
            for k in range(KC):
                op("pe", lambda e, pb=pb, k=k: e.matmul(pb[:, 0:n], lhsT=onesm[:, :], rhs=src[:, k, 0:n],
                                                        start=(k == 0), stop=(k == KC - 1)),
                   [src_reg, r_const], [pr], mark=(k == KC - 1))
            rs, rr = rot("rstd", rstd, r_rstd)
            op("act", lambda e, pb=pb, rs=rs: e.activation(out=rs[:, 0:n], in_=pb[:, 0:n], func=AF.Sqrt, bias=epsr[:, 0:1]),
               [pr, r_const], [rr])
            op("dve", lambda e, rs=rs: e.reciprocal(out=rs[:, 0:n], in_=rs[:, 0:n]), [rr], [rr])
            return rs, rr

        cur = {}

        def prenorm(l, i, gi):
            g0, n = GROUPS[gi]
            A_xn.recarve()
            r_xsq = A_xn.reg("xsq")
            op("act", lambda e: e.activation(out=xnT[:, :, 0:n], in_=xT[:, :, g0:g0 + n], func=AF.Square),
               xregs(gi), [r_xsq])
            rs, rr = norm_stats(r_xsq, n, src=xnT)
            A_xn.recarve()
            r_xn = A_xn.reg("xn")
            cur["r_xn"] = r_xn
            for k in range(KC):
                col = l * 48 + i * 8 + k
                op("dve", lambda e, k=k, col=col, rs=rs: e.scalar_tensor_tensor(
                    out=xnT[:, k, 0:n], in0=xT[:, k, g0:g0 + n], scalar=gT[:, col:col + 1], in1=rs[:, 0:n],
                    op0=ALU.mult, op1=ALU.mult), xregs(gi) + [rr, r_const], [r_xn], nowaw=True)
            return r_xn

        ob3 = obuf[:, :].rearrange("p (k n) -> p k n", k=KC)

        def prenorm_stats_m(l, i, gi):
            g0, n = GROUPS[gi]
            op("act", lambda e: e.activation(out=sq[:, :, 0:n], in_=xT[:, :, g0:g0 + n], func=AF.Square), xregs(gi), [r_sq])
            return norm_stats(r_sq, n)

        def prenorm_apply_m(l, i, gi, rs, rr):
            g0, n = GROUPS[gi]
            A_xn.recarve()
            r_xn = A_xn.reg("xn")
            for k in range(KC):
                col = l * 48 + i * 8 + k
                op("dve", lambda e, k=k, col=col, rs=rs: e.scalar_tensor_tensor(
                    out=xnT[:, k, 0:n], in0=xT[:, k, g0:g0 + n], scalar=gT[:, col:col + 1], in1=rs[:, 0:n],
                    op0=ALU.mult, op1=ALU.mult), xregs(gi) + [rr, r_const], [r_xn], nowaw=True)
            return r_xn

        def postnorm(l, i, gi, r_ob, half):
            g0, n = GROUPS[gi]
            rs, rr = norm_stats(r_sq, n)
            gsrc = gTh if half else gT
            for d0 in range(0, KC, 2):
                tbs = []
                for d in (d0, d0 + 1):
                    col = l * 48 + i * 8 + d
                    tb, tr = rot("tmp", tmpb, r_tmp)
                    tbs.append((d, tb, tr))
                    op("dve", lambda e, d=d, col=col, tb=tb, rs=rs: e.scalar_tensor_tensor(
                        out=tb[:, 0:n], in0=ob3[:, d, 0:n], scalar=gsrc[:, col:col + 1], in1=rs[:, 0:n],
                        op0=ALU.mult, op1=ALU.mult), [r_ob, rr, r_const], [tr])
                for (d, tb, tr) in tbs:
                    op("dve", lambda e, d=d, tb=tb: e.tensor_tensor(
                        out=xT[:, d, g0:g0 + n], in0=xT[:, d, g0:g0 + n], in1=tb[:, 0:n], op=ALU.add),
                       [tr] + xregs(gi), xregs(gi), nowaw=True)

        def proj_out_evac(pb, pr, d, n, r_ob):
            op("dve", lambda e, pb=pb, d=d: e.tensor_copy(out=ob3[:, d, 0:n], in_=pb[:, 0:n]), [pr], [r_ob], nowaw=True)
            op("act", lambda e, d=d: e.activation(out=sq[:, d, 0:n], in_=ob3[:, d, 0:n], func=AF.Square), [r_ob], [r_sq], nowaw=True)

        def mm8(pb_ap, pr, w, rw, col0, ncol, rhs_fn, rhs_regs, n=None):
            for k in range(KC):
                op("pe", lambda e, k=k: e.matmul(pb_ap, lhsT=w[:, k, col0:col0 + ncol], rhs=rhs_fn(k),
                                                 start=(k == 0), stop=(k == KC - 1)),
                   [rw] + rhs_regs, [pr], mark=(k == KC - 1))

        def fsegs(g):
            return [(g * 512, 512, 0, r_xT[g]), (NPR + 32 * g, 32, 512, r_xTs[g])]

        def f_stats(src, src_reg):
            rs, rr = rot("rstd", rstd, r_rstd)
            for (b0, bn) in FBLK:
                pb, pr = palloc()
                for k in range(KC):
                    op("pe", lambda e, pb=pb, k=k, b0=b0, bn=bn: e.matmul(pb[:, 0:bn], lhsT=onesm[:, :], rhs=src[:, k, b0:b0 + bn],
                                                                          start=(k == 0), stop=(k == KC - 1)),
                       [src_reg, r_const], [pr], mark=(k == KC - 1))
                op("act", lambda e, pb=pb, rs=rs, b0=b0, bn=bn: e.activation(out=rs[:, b0:b0 + bn], in_=pb[:, 0:bn], func=AF.Sqrt,
                                                                            bias=epsr[:, 0:1]), [pr, r_const], [rr])
            op("dve", lambda e, rs=rs: e.reciprocal(out=rs[:, 0:FN], in_=rs[:, 0:FN]), [rr], [rr])
            return rs, rr

        def f_prenorm_stats(l, i, g):
            for (c0, n, o, rx) in fsegs(g):
                op("act", lambda e, c0=c0, n=n, o=o: e.activation(out=sq[:, :, o:o + n], in_=xT[:, :, c0:c0 + n], func=AF.Square),
                   [rx], [r_sq])
            return f_stats(sq, r_sq)

        def f_prenorm_apply(l, i, g, rs, rr):
            A_xn.recarve()
            r_xn = A_xn.reg("xn")
            for k in range(KC):
                col = l * 48 + i * 8 + k
                for (c0, n, o, rx) in fsegs(g):
                    op("dve", lambda e, k=k, col=col, rs=rs, c0=c0, n=n, o=o: e.scalar_tensor_tensor(
                        out=xnT[:, k, o:o + n], in0=xT[:, k, c0:c0 + n], scalar=gT[:, col:col + 1], in1=rs[:, o:o + n],
                        op0=ALU.mult, op1=ALU.mult), [rx, rr, r_const], [r_xn], nowaw=True)
            return r_xn

        def f_prenorm(l, i, g):
            rs, rr = f_prenorm_stats(l, i, g)
            return f_prenorm_apply(l, i, g, rs, rr)

        def ffn(l, which, g, r_xn, next_pre=None):
            ipre = 0 if which == 1 else 4
            A_big.recarve()
            r_hT = [A_big.reg(f"hT{f}") for f in range(FC)]
            hT = big[:, 0:FC * FN].rearrange("p (f n) -> p f n", f=FC)
            for fp in range(11):
                wg, rg = s4_next()
                wu, ru = s4_next()
                for fi in range(2):
                    f = 2 * fp + fi
                    for (b0, bn) in FBLK:
                        pg, prg = palloc()
                        pu, pru = palloc()
                        mm8(pg[:, 0:bn], prg, wg, rg, fi * P, P, lambda k, b0=b0, bn=bn: xnT[:, k, b0:b0 + bn], [r_xn])
                        mm8(pu[:, 0:bn], pru, wu, ru, fi * P, P, lambda k, b0=b0, bn=bn: xnT[:, k, b0:b0 + bn], [r_xn])
                        sgt, sgr = rot("sg", sgb, r_sg)
                        op("act", lambda e, pg=pg, sgt=sgt, bn=bn: e.activation(out=sgt[:, 0:bn], in_=pg[:, 0:bn], func=AF.Silu),
                           [prg], [sgr])
                        op("dve", lambda e, pu=pu, sgt=sgt, f=f, b0=b0, bn=bn: e.tensor_tensor(
                            out=hT[:, f, b0:b0 + bn], in0=sgt[:, 0:bn], in1=pu[:, 0:bn], op=ALU.mult), [pru, sgr], [r_hT[f]], nowaw=True)
                s4_issue()
                s4_issue()
                if fp == 5 and next_pre is not None:
                    nst = f_prenorm_stats(*next_pre)
                if pending_post:
                    pending_post.pop(0)()
            nxt = f_prenorm_apply(*next_pre, *nst) if next_pre is not None else None
            A_obuf.recarve()
            r_ob = A_obuf.reg("ob")
            for dp in range(4):
                skip = (which == 1 and g == NFG - 1 and dp >= 2 and 'mx' in PH)
                acc = {}
                for di in range(2):
                    for bi in range(2):
                        acc[(di, bi)] = palloc()
                for hh in range(2):
                    wd, rd_ = s11_next()
                    for di in range(2):
                        for bi, (b0, bn) in enumerate(FBLK):
                            pb, pr = acc[(di, bi)]
                            for fl in range(11):
                                f = hh * 11 + fl
                                op("pe", lambda e, pb=pb, f=f, fl=fl, di=di, wd=wd, b0=b0, bn=bn: e.matmul(
                                    pb[:, 0:bn], lhsT=wd[:, fl, di * P:(di + 1) * P], rhs=hT[:, f, b0:b0 + bn],
                                    start=(f == 0), stop=(f == FC - 1)), [rd_, r_hT[f]], [pr], mark=(fl == 10))
                    if not skip:
                        s11_issue()
                for di in range(2):
                    d = 2 * dp + di
                    for bi, (b0, bn) in enumerate(FBLK):
                        pb, pr = acc[(di, bi)]
                        op("dve", lambda e, pb=pb, d=d, b0=b0, bn=bn: e.tensor_copy(out=ob3[:, d, b0:b0 + bn], in_=pb[:, 0:bn]), [pr], [r_ob], nowaw=True)
                        op("act", lambda e, d=d, b0=b0, bn=bn: e.activation(out=sq[:, d, b0:b0 + bn], in_=ob3[:, d, b0:b0 + bn], func=AF.Square),
                           [r_ob], [r_sq], nowaw=True)
            rs, rr = f_stats(sq, r_sq)

            def post_chunk(d):
                col = l * 48 + (ipre + 1) * 8 + d
                tb, tr = rot("tmp", tmpb, r_tmp)
                op("dve", lambda e, d=d, col=col, tb=tb, rs=rs: e.scalar_tensor_tensor(
                    out=tb[:, 0:FN], in0=ob3[:, d, 0:FN], scalar=gTh[:, col:col + 1], in1=rs[:, 0:FN],
                    op0=ALU.mult, op1=ALU.mult), [r_ob, rr, r_const], [tr])
                for (c0, n, o, rx) in fsegs(g):
                    op("dve", lambda e, d=d, tb=tb, c0=c0, n=n, o=o: e.tensor_tensor(
                        out=xT[:, d, c0:c0 + n], in0=xT[:, d, c0:c0 + n], in1=tb[:, o:o + n], op=ALU.add),
                       [tr, rx], [rx], nowaw=True)

            for d in range(KC):
                pending_post.append(lambda d=d: post_chunk(d))
            return nxt

        mx = {}

        def mixer_begin(l):
            for a_ in A_s11:
                a_.recarve()
            A_big.recarve()
            A_obuf.recarve()
            mx["kT"] = s11[0][:, 0:2304]
            mx["r_kT"] = multi_reg("kT", A_s11[0:2])
            mx["Vtm"] = s11[0][:, 2304:4608].rearrange("p (t c) -> p t c", t=18)
            mx["r_V"] = multi_reg("Vtm", A_s11[0:2])
            mx["KsT"] = s11[1][:, 0:2048].rearrange("p (s k) -> p s k", s=16)
            mx["r_KsT"] = multi_reg("KsT", A_s11[2:4])
            mx["Vs"] = s11[1][:, 2048:4096].rearrange("p (s c) -> p s c", s=16)
            mx["r_Vs"] = multi_reg("Vs", A_s11[2:4])
            mx["pus"] = s11[1][:, 4096:5632].bitcast(F32).rearrange("p (c s t) -> p c s t", c=2, s=16)
            mx["r_pus"] = multi_reg("pus", A_s11[2:4])
            mx["qT"] = big[:, 0:2048].rearrange("p (c n) -> p c n", c=4)
            mx["r_qT"] = A_big.reg("qT")
            mx["mixT"] = big[:, 2048:6144].rearrange("p (c n) -> p c n", c=8)
            mx["r_mix"] = [A_big.reg(f"mix{c}") for c in range(8)]
            mx["gx"] = big[:, 6144:7228].rearrange("p (c n) -> p c n", c=2)
            mx["r_gx"] = A_big.reg("gx")
            mx["PTp"] = big[:, 7228:8252]
            mx["r_PTp"] = A_big.reg("PTp")
            mx["PTc"] = big[:, 8252:9276]
            mx["r_PTc"] = A_big.reg("PTc")
            mx["gxs"] = big[:, 9276:10492].rearrange("p (c s t) -> p c s t", c=2, s=16)
            mx["r_gxs"] = A_big.reg("gxs")
            kT, Vtm, gx = mx["kT"], mx["Vtm"], mx["gx"]
            op("dve", lambda e: e.memset(kT[:, 0:128], 0.0), [], [mx["r_kT"]])
            op("dve", lambda e: e.memset(Vtm[:, 0, :], 0.0), [], [mx["r_V"]])
            op("dve", lambda e: e.memset(gx[:, :, 0:30], 0.0), [], [mx["r_gx"]])
            op("dve", lambda e: e.memset(puc[:, :, :], 0.0), [], [r_puc])
            op("dve", lambda e: e.memset(mx["pus"][:, :, :, 0:1], 0.0), [], [mx["r_pus"]])
            if 'nosprep' in DBG:
                return
            Vs, KsT, pus, gxs = mx["Vs"], mx["KsT"], mx["pus"], mx["gxs"]
            for h in range(2):
                dma("pool", Vs[:, 8 * h:8 * h + 8, :], sv_d[l, 8 * h:8 * h + 8].rearrange("s k c -> k s c"), [], [mx["r_Vs"]], d_vs)
            stK = [obuf[:, h * 1024:(h + 1) * 1024].rearrange("p (s c) -> p s c", s=8) for h in range(2)]
            r_stK = [A_obuf.reg("stK0"), A_obuf.reg("stK1")]
            for h in range(2):
                dma("sp", stK[h], sk_d[l, 8 * h:8 * h + 8].rearrange("s k c -> k s c"), [], [r_stK[h]], d_stK[h])
                for q4 in range(2):
                    pb, pr = palloc()
                    for ss in range(4):
                        s_ = q4 * 4 + ss
                        op("pe", lambda e, pb=pb, ss=ss, s_=s_, h=h: e.transpose(
                            out=pb[:, ss * P:(ss + 1) * P], in_=stK[h][:, s_, :], identity=ident[:, :]),
                           [r_stK[h], r_const], [pr], mark=(ss == 3))
                    s0 = h * 8 + q4 * 4
                    op("act", lambda e, pb=pb, s0=s0: e.activation(
                        out=KsT[:, s0:s0 + 4, :], in_=pb[:, :].rearrange("p (a b) -> p a b", a=4), func=AF.Copy),
                       [pr], [mx["r_KsT"]])
            stC = obuf[:, 2048:2304]
            r_stC = A_obuf.reg("stC")
            for i4 in range(4):
                dma("sp", stC[0:120, :], sc_d[l, 4 * i4:4 * i4 + 4].rearrange("s r c -> (s r) c"), [], [r_stC], d_stC)
                for cc in range(2):
                    pb, pr = palloc()
                    op("pe", lambda e, pb=pb, cc=cc: e.transpose(out=pb[:, 0:120], in_=stC[0:120, cc * P:(cc + 1) * P],
                                                                 identity=ident[0:120, 0:120]), [r_stC, r_const], [pr])
                    op("dve", lambda e, pb=pb, cc=cc, i4=i4: e.tensor_copy(
                        out=gxs[:, cc, 4 * i4:4 * i4 + 4, 0:30], in_=pb[:, 0:120].rearrange("p (s r) -> p s r", s=4)),
                       [pr], [mx["r_gxs"]])
            for i8 in range(2):
                dma("sp", stC[0:120, :], sp_d[l, 8 * i8:8 * i8 + 8].rearrange("s r c -> (s r) c"), [], [r_stC], d_stC)
                for pc in range(2):
                    pb, pr = palloc()
                    op("pe", lambda e, pb=pb, pc=pc: e.transpose(out=pb[:, 0:120], in_=stC[0:120, pc * P:(pc + 1) * P],
                                                                 identity=ident[0:120, 0:120]), [r_stC, r_const], [pr])
                    op("dve", lambda e, pb=pb, pc=pc, i8=i8: e.tensor_copy(
                        out=pus[:, pc, 8 * i8:8 * i8 + 8, 1:16], in_=pb[:, 0:120].rearrange("p (s r) -> p s r", s=8)),
                       [pr], [mx["r_pus"]])

        def out_rows(si, dst_ap, src_ap, extra_rd=()):
            dma("sp", dst_ap, src_ap, [r_stgo[si]] + list(extra_rd), [], d_stgo[si])

        def mixer(l, gi):
            g0, n = GROUPS[gi]
            NGm = NG
            sample = (gi == 4)
            last_prompt = (gi == 3)
            kT, Vtm, qT, mixT, gx = mx["kT"], mx["Vtm"], mx["qT"], mx["mixT"], mx["gx"]
            r_kT, r_V, r_qT, r_mix, r_gx = mx["r_kT"], mx["r_V"], mx["r_qT"], mx["r_mix"], mx["r_gx"]
            gxs, pus, KsT, Vs, PTp, PTc = mx["gxs"], mx["pus"], mx["KsT"], mx["Vs"], mx["PTp"], mx["PTc"]
            r_xn = mx.pop("pre", None)
            if r_xn is None:
                r_xn = prenorm(l, 2, gi)
            A_obuf.recarve()
            pu_g = obuf[:, 0:1056].rearrange("p (c n) -> p c n", c=2)
            r_pu = A_obuf.reg("pu")
            cacc = obuf[:, 1056:2080].rearrange("p (c n) -> p c n", c=2)
            r_cacc = [A_obuf.reg("cacc0"), A_obuf.reg("cacc1")]
            sA = obuf[:, 2080:2608]
            sB = obuf[:, 2608:3136]
            r_sA, r_sB = A_obuf.reg("sA"), A_obuf.reg("sB")
            g32 = obuf[:, 3136:3392].rearrange("p (c n) -> p c n", c=2)
            r_g32 = A_obuf.reg("g32")
            p32 = obuf[:, 3392:3648].rearrange("p (c n) -> p c n", c=2)
            r_p32 = A_obuf.reg("p32")
            if not sample:
                op("dve", lambda e: e.tensor_copy(out=pu_g[:, :, 0:16], in_=puc[:, :, :]), [r_puc], [r_pu])
                if gi > 0:
                    op("dve", lambda e: e.tensor_copy(out=gx[:, :, 0:30], in_=gx[:, :, 512:542]), [r_gx], [r_gx])
            xn = lambda k: xnT[:, k, 0:n]
            for pi in range(2):
                w, rw = s4_next()
                for ci in range(2):
                    c = 2 * pi + ci
                    pb, pr = palloc()
                    mm8(pb[:, 0:n], pr, w, rw, ci * P, P, xn, [r_xn])
                    op("act", lambda e, pb=pb, c=c: e.activation(out=qT[:, c, 0:n], in_=pb[:, 0:n], func=AF.Copy, scale=0.125),
                       [pr], [r_qT])
                s4_issue()
            if ML < 2:
                for _ in range(8):
                    s4_next(); s4_issue()
                return
            w, rw = s4_next()
            pb, pr = palloc()
            mm8(pb[:, 0:n], pr, w, rw, 0, P, xn, [r_xn])
            op("dve", lambda e, pb=pb: e.tensor_copy(out=kT[:, 128 + g0:128 + g0 + n], in_=pb[:, 0:n]), [pr], [r_kT])
            for tl in range(n // P):
                tile = g0 // P + tl
                full = sample or (tile == 15)
                pv, prv = palloc()
                c0 = 0 if full else P
                for k in range(KC):
                    op("pe", lambda e, k=k, pv=pv, tl=tl, c0=c0, w=w: e.matmul(
                        pv[:, c0:256], lhsT=xnT[:, k, tl * P:(tl + 1) * P], rhs=w[:, k, c0:256],
                        start=(k == 0), stop=(k == KC - 1)), [rw, r_xn], [prv], mark=(k == KC - 1))
                op("act", lambda e, pv=pv, tile=tile: e.activation(out=Vtm[:, 1 + tile, :], in_=pv[:, P:256], func=AF.Copy),
                   [prv], [r_V])
                if full:
                    op("act", lambda e, pv=pv: e.activation(out=stgo[:, 0:256], in_=pv[:, 0:256], func=AF.Copy), [prv], [r_stgo[0]])
                    if sample:
                        for s_ in range(16):
                            out_rows(0, ks_d[l, s_, 120:128, :], stgo[8 * s_:8 * s_ + 8, 0:128])
                            out_rows(0, vs_d[l, s_, 120:128, :], stgo[8 * s_:8 * s_ + 8, 128:256])
                    else:
                        out_rows(0, kp_d[l], stgo[:, 0:128])
                        out_rows(0, vp_d[l], stgo[:, 128:256])
            s4_issue()
            if ML < 3:
                for _ in range(7):
                    s4_next(); s4_issue()
                return
            for cc in range(2):
                w, rw = s4_next()
                pa, pra = palloc()
                pg, prg = palloc()
                mm8(pa[:, 0:n], pra, w, rw, 0, P, xn, [r_xn])
                mm8(pg[:, 0:n], prg, w, rw, P, P, xn, [r_xn])
                sgt, sgr = rot("sg", sgb, r_sg)
                op("act", lambda e, pg=pg, sgt=sgt: e.activation(out=sgt[:, 0:n], in_=pg[:, 0:n], func=AF.Sigmoid),
                   [prg], [sgr])
                if sample:
                    op("dve", lambda e, pa=pa, sgt=sgt, cc=cc: e.tensor_tensor(
                        out=g32[:, cc, :], in0=sgt[:, 0:P], in1=pa[:, 0:P], op=ALU.mult), [pra, sgr], [r_g32])
                    op("dve", lambda e, cc=cc: e.tensor_copy(
                        out=gxs[:, cc, :, 30:38], in_=g32[:, cc, :].rearrange("p (s t) -> p s t", s=16)),
                       [r_g32], [mx["r_gxs"]])
                else:
                    op("dve", lambda e, pa=pa, sgt=sgt, cc=cc: e.tensor_tensor(
                        out=gx[:, cc, 30:30 + n], in0=sgt[:, 0:n], in1=pa[:, 0:n], op=ALU.mult), [pra, sgr], [r_gx])
                    if last_prompt:
                        op("dve", lambda e, pa=pa, sgt=sgt, cc=cc: e.tensor_tensor(
                            out=g32[:, cc, :], in0=sgt[:, n - P:n], in1=pa[:, n - P:n], op=ALU.mult), [pra, sgr], [r_g32])
                s4_issue()
            w, rw = s4_next()
            for pc in range(2):
                pb, pr = palloc()
                mm8(pb[:, 0:n], pr, w, rw, pc * P, P, xn, [r_xn])
                if sample:
                    op("act", lambda e, pb=pb, pc=pc: e.activation(out=p32[:, pc, :], in_=pb[:, 0:P], func=AF.Copy),
                       [pr], [r_p32])
                    op("dve", lambda e, pc=pc: e.tensor_copy(
                        out=pus[:, pc, :, 16:24], in_=p32[:, pc, :].rearrange("p (s t) -> p s t", s=16)),
                       [r_p32], [mx["r_pus"]])
                else:
                    op("act", lambda e, pb=pb, pc=pc: e.activation(out=pu_g[:, pc, 16:16 + n], in_=pb[:, 0:n], func=AF.Copy),
                       [pr], [r_pu])
            s4_issue()

            if ML < 4:
                for _ in range(4):
                    s4_next(); s4_issue()
                return
            A_xn.recarve()
            cbf = xnT[:, 0:2, :]
            csq = xnT[:, 2:4, :]
            cs_ = xnT[:, 4:6, :]
            dbf = xnT[:, 6, :]
            r_cbf, r_csq, r_cs, r_dbf = A_xn.reg("cbf"), A_xn.reg("csq"), A_xn.reg("cs"), A_xn.reg("dbf")

            def gsrc(cc, j):
                if sample:
                    return gxs[:, cc, :, j:j + 8]
                return gx[:, cc, j:j + n]

            def cdst(cc):
                if sample:
                    return cacc[:, cc, 0:P].rearrange("p (s t) -> p s t", s=16)
                return cacc[:, cc, 0:n]

            r_g = mx["r_gxs"] if sample else r_gx
            for cc in range(2):
                pc_, prc_ = palloc()
                for j in range(31):
                    dt_, dr_ = rot("dg", dring, r_dring)
                    wcol = dwT[:, cc, l * 31 + j:l * 31 + j + 1]
                    op("act", lambda e, dt_=dt_, wcol=wcol: e.activation(out=dt_[:, :], in_=identb[:, :], func=AF.Copy, scale=wcol),
                       [r_const], [dr_])
                    rhs_ = gsrc(cc, j)
                    op("pe", lambda e, pc_=pc_, dt_=dt_, rhs_=rhs_, j=j: e.matmul(pc_[:, 0:n], lhsT=dt_[:, :], rhs=rhs_,
                                                                                  start=(j == 0), stop=(j == 30)),
                       [dr_, r_g], [prc_], mark=True)
                bcol = vT[:, cc, l * 4 + 0:l * 4 + 1]
                op("act", lambda e, pc_=pc_, cc=cc, bcol=bcol: e.activation(out=cacc[:, cc, 0:n], in_=pc_[:, 0:n], func=AF.Identity, bias=bcol),
                   [prc_, r_const], [r_cacc[cc]])
            def normalize(pO, prO, c, qcols):
                Rt, Rr = rot("R", Rb, r_R)
                col = l * 4 + c
                op("dve", lambda e: e.tensor_scalar(out=Rt[:, :], in0=pO[:, P:2 * P], scalar1=esc[:, col:col + 1], scalar2=None,
                                                    op0=ALU.add), [prO, r_const], [Rr])
                op("dve", lambda e: e.reciprocal(out=Rt[:, :], in_=Rt[:, :]), [Rr], [Rr])
                op("dve", lambda e: e.tensor_tensor(out=mixT[:, c, qcols], in0=pO[:, 0:P], in1=Rt[:, :], op=ALU.mult),
                   [prO, Rr], [r_mix[c]])

            norm_pending = []

            def normalize2(items, qcols):
                Rs = [rot("R", Rb, r_R) for _ in items]
                for (pO, prO, c), (Rt, Rr) in zip(items, Rs):
                    col = l * 4 + c
                    op("dve", lambda e, Rt=Rt, pO=pO, col=col: e.tensor_scalar(out=Rt[:, :], in0=pO[:, P:2 * P], scalar1=esc[:, col:col + 1],
                                                                              scalar2=None, op0=ALU.add), [prO, r_const], [Rr])
                for (pO, prO, c), (Rt, Rr) in zip(items, Rs):
                    op("dve", lambda e, Rt=Rt: e.reciprocal(out=Rt[:, :], in_=Rt[:, :]), [Rr], [Rr])
                for (pO, prO, c), (Rt, Rr) in zip(items, Rs):
                    op("dve", lambda e, Rt=Rt, pO=pO, c=c: e.tensor_tensor(out=mixT[:, c, qcols], in0=pO[:, 0:P], in1=Rt[:, :], op=ALU.mult),
                       [prO, Rr], [r_mix[c]])

            if not sample:
                for j in range(n // P):
                    qb = g0 // P + j
                    for cp in range(2):
                        pSs = [palloc(), palloc()]
                        for hi in range(2):
                            pS, prS = pSs[hi]
                            t0 = (cp * 2 + hi) * 512
                            if qb == 0:
                                for seg in range(4):
                                    rhs_ = negt[:, :] if seg % 2 == 0 else maskp[:, t0 + seg * P:t0 + (seg + 1) * P]
                                    op("pe", lambda e, pS=pS, seg=seg, rhs_=rhs_: e.matmul(
                                        pS[:, seg * P:(seg + 1) * P], lhsT=identb[:, :], rhs=rhs_,
                                        start=(seg == 0), stop=False, skip_group_check=True), [r_const], [prS], mark=False)
                            else:
                                op("pe", lambda e, pS=pS, t0=t0: e.matmul(pS[:, :], lhsT=identb[:, :], rhs=maskp[:, t0:t0 + 512],
                                                                          start=True, stop=False), [r_const], [prS], mark=False)
                        for c2 in range(2):
                            c = cp * 2 + c2
                            for kb in range(2):
                                kc0 = qb * P + kb * P
                                last = (c2 == 1 and kb == 1)
                                for hi in range(2):
                                    pS, prS = pSs[hi]
                                    seg = c2 * 2 + kb
                                    sgc = (qb == 0)
                                    op("pe", lambda e, pS=pS, c=c, hi=hi, seg=seg, kc0=kc0, j=j, last=last, sgc=sgc: e.matmul(
                                        pS[:, seg * P:(seg + 1) * P],
                                        lhsT=kT[hi * 64:(hi + 1) * 64, kc0:kc0 + P], rhs=qT[hi * 64:(hi + 1) * 64, c, j * P:(j + 1) * P],
                                        start=False, stop=last, skip_group_check=sgc), [r_kT, r_qT], [prS], mark=last)
                        PTs = []
                        for hi in range(2):
                            pS, prS = pSs[hi]
                            PT, rPT = rot("PT", PTb, r_PT)
                            op("act", lambda e, pS=pS, PT=PT: e.activation(out=PT[:, :], in_=pS[:, :], func=AF.Exp), [prS], [rPT])
                            PTs.append((PT, rPT))
                        for c2 in range(2):
                            c = cp * 2 + c2
                            pO, prO = palloc()
                            for hi in range(2):
                                PT, rPT = PTs[hi]
                                for which_ in range(2):
                                    for kb in range(2):
                                        last = (hi == 1 and which_ == 1 and kb == 1)
                                        seg = c2 * 2 + kb
                                        lhs = Vtm[:, qb + kb, hi * 64:(hi + 1) * 64] if which_ == 0 else onesb[:, 0:64]
                                        op("pe", lambda e, pO=pO, hi=hi, which_=which_, kb=kb, lhs=lhs, PT=PT, seg=seg: e.matmul(
                                            pO[hi * 64:(hi + 1) * 64, which_ * P:(which_ + 1) * P], lhsT=lhs,
                                            rhs=PT[:, seg * P:(seg + 1) * P],
                                            start=(kb == 0), stop=(kb == 1)), [r_V, rPT, r_const], [prO], mark=last)
                            norm_pending.append((pO, prO, c))
                        normalize2(norm_pending, slice(j * P, (j + 1) * P))
                        del norm_pending[:]
                    if j == 2 and gi + 1 < NGm:
                        mx["pre_st"] = prenorm_stats_m(l, 2, gi + 1)
            else:
                pSp = [palloc(), palloc()]
                for kvh in range(2):
                    pS, prS = pSp[kvh]
                    op("pe", lambda e, pS=pS, kvh=kvh: e.matmul(pS[:, :], lhsT=identb[:, :], rhs=msp[:, kvh * 512:(kvh + 1) * 512],
                                                                start=True, stop=False), [r_const], [prS], mark=False)
                for s_ in range(16):
                    for kvh in range(2):
                        pS, prS = pSp[kvh]
                        last = (s_ == 15)
                        op("pe", lambda e, pS=pS, s_=s_, kvh=kvh, last=last: e.matmul(
                            pS[:, s_ * 32:(s_ + 1) * 32],
                            lhsT=KsT[kvh * 64:(kvh + 1) * 64, s_, :], rhs=qT[kvh * 64:(kvh + 1) * 64, :, s_ * 8:(s_ + 1) * 8],
                            start=False, stop=last), [mx["r_KsT"], r_qT], [prS], mark=last)
                for kvh in range(2):
                    pS, prS = pSp[kvh]
                    op("act", lambda e, pS=pS, kvh=kvh: e.activation(out=PTp[:, kvh * 512:(kvh + 1) * 512], in_=pS[:, :], func=AF.Exp),
                       [prS], [mx["r_PTp"]])
                KnT = kT[:, 128 + g0:128 + g0 + P]
                pSc = [palloc(), palloc()]
                for hi in range(2):
                    pS, prS = pSc[hi]
                    op("pe", lambda e, pS=pS, hi=hi: e.matmul(pS[:, :], lhsT=identb[:, :], rhs=msc[:, hi * 512:(hi + 1) * 512],
                                                              start=True, stop=False), [r_const], [prS], mark=False)
                for c in range(4):
                    for hi in range(2):
                        pS, prS = pSc[hi]
                        last = (c == 3)
                        op("pe", lambda e, pS=pS, c=c, hi=hi, last=last: e.matmul(
                            pS[:, c * P:(c + 1) * P],
                            lhsT=KnT[hi * 64:(hi + 1) * 64, :], rhs=qT[hi * 64:(hi + 1) * 64, c, 0:P],
                            start=False, stop=last), [r_kT, r_qT], [prS], mark=last)
                for hi in range(2):
                    pS, prS = pSc[hi]
                    op("act", lambda e, pS=pS, hi=hi: e.activation(out=PTc[:, hi * 512:(hi + 1) * 512], in_=pS[:, :], func=AF.Exp),
                       [prS], [mx["r_PTc"]])
                pOs, prOs = palloc()
                pSm, prSm = palloc()
                for (pb_, pr_, isO) in ((pOs, prOs, True), (pSm, prSm, False)):
                    pb4 = pb_[:, :].rearrange("p (g q) -> p g q", g=4)
                    pb5 = pb_[:, :].rearrange("p (g s t) -> p g s t", g=4, s=16)
                    for hi in range(2):
                        for g in range(4):
                            lhs = Vtm[:, 17, hi * 64:(hi + 1) * 64] if isO else onesb[:, 0:64]
                            op("pe", lambda e, pb4=pb4, hi=hi, g=g, lhs=lhs: e.matmul(
                                pb4[hi * 64:(hi + 1) * 64, g, :], lhsT=lhs, rhs=PTc[:, hi * 512 + g * P:hi * 512 + (g + 1) * P],
                                start=(g == 0), stop=False, skip_group_check=True),
                               [r_V, mx["r_PTc"], r_const], [pr_], mark=False)
                        for s_ in range(16):
                            lhs = Vs[:, s_, hi * 64:(hi + 1) * 64] if isO else onesb[:, 0:64]
                            for g in range(4):
                                last = (hi == 1 and s_ == 15 and g == 3)
                                c0 = hi * 512 + s_ * 32 + g * 8
                                op("pe", lambda e, pb5=pb5, hi=hi, s_=s_, lhs=lhs, g=g, c0=c0: e.matmul(
                                    pb5[hi * 64:(hi + 1) * 64, g, s_, :], lhsT=lhs, rhs=PTp[:, c0:c0 + 8],
                                    start=False, stop=(s_ == 15), skip_group_check=True),
                                   [mx["r_Vs"], mx["r_PTp"], r_const], [pr_], mark=last)
                for c in range(4):
                    Rt, Rr = rot("R", Rb, r_R)
                    col = l * 4 + c
                    op("dve", lambda e, Rt=Rt, c=c, col=col: e.tensor_scalar(
                        out=Rt[:, :], in0=pSm[:, c * P:(c + 1) * P], scalar1=esc[:, col:col + 1], scalar2=None,
                        op0=ALU.add), [prSm, r_const], [Rr])
                    op("dve", lambda e, Rt=Rt: e.reciprocal(out=Rt[:, :], in_=Rt[:, :]), [Rr], [Rr])
                    op("dve", lambda e, Rt=Rt, c=c: e.tensor_tensor(out=mixT[:, c, 0:P], in0=pOs[:, c * P:(c + 1) * P], in1=Rt[:, :],
                                                                   op=ALU.mult), [prOs, Rr], [r_mix[c]])

            if ML < 5:
                for _ in range(4):
                    s4_next(); s4_issue()
                return
            for cc in range(2):
                op("act", lambda e, cc=cc: e.activation(out=cbf[:, cc, 0:n], in_=cacc[:, cc, 0:n], func=AF.Copy),
                   [r_cacc[cc]], [r_cbf])
                op("act", lambda e, cc=cc: e.activation(out=csq[:, cc, 0:n], in_=cacc[:, cc, 0:n], func=AF.Square),
                   [r_cacc[cc]], [r_csq])
            pm, prm = palloc()
            pq, prq = palloc()
            for cc in range(2):
                op("pe", lambda e, cc=cc: e.matmul(pm[:, 0:n], lhsT=ones256[:, :], rhs=cbf[:, cc, 0:n], start=(cc == 0), stop=(cc == 1)),
                   [r_cbf, r_const], [prm], mark=(cc == 1))
            for cc in range(2):
                op("pe", lambda e, cc=cc: e.matmul(pq[:, 0:n], lhsT=ones256[:, :], rhs=csq[:, cc, 0:n], start=(cc == 0), stop=(cc == 1)),
                   [r_csq, r_const], [prq], mark=(cc == 1))
            mt, mr = rot("tmp", tmpb, r_tmp)
            vt, vr = rot("tmp", tmpb, r_tmp)
            op("act", lambda e: e.activation(out=mt[:, 0:n], in_=pm[:, 0:n], func=AF.Copy), [prm], [mr])
            op("dve", lambda e: e.tensor_tensor(out=vt[:, 0:n], in0=mt[:, 0:n], in1=mt[:, 0:n], op=ALU.mult), [mr], [vr])
            op("dve", lambda e: e.tensor_tensor(out=vt[:, 0:n], in0=pq[:, 0:n], in1=vt[:, 0:n], op=ALU.subtract), [prq, vr], [vr])
            op("act", lambda e: e.activation(out=vt[:, 0:n], in_=vt[:, 0:n], func=AF.Sqrt, bias=epsl[:, 0:1]), [vr, r_const], [vr])
            op("dve", lambda e: e.reciprocal(out=vt[:, 0:n], in_=vt[:, 0:n]), [vr], [vr])
            for cc in range(2):
                op("dve", lambda e, cc=cc: e.tensor_tensor(out=cacc[:, cc, 0:n], in0=cacc[:, cc, 0:n], in1=mt[:, 0:n], op=ALU.subtract),
                   [r_cacc[cc], mr], [r_cacc[cc]])
                op("dve", lambda e, cc=cc: e.tensor_tensor(out=cacc[:, cc, 0:n], in0=cacc[:, cc, 0:n], in1=vt[:, 0:n], op=ALU.mult),
                   [r_cacc[cc], vr], [r_cacc[cc]])
                op("act", lambda e, cc=cc: e.activation(out=cs_[:, cc, 0:n], in_=cacc[:, cc, 0:n], func=AF.Silu,
                                                        scale=vT[:, cc, l * 4 + 1:l * 4 + 2], bias=vT[:, cc, l * 4 + 2:l * 4 + 3]),
                   [r_cacc[cc], r_const], [r_cs])
            for oc in range(2):
                pb, pr = palloc()
                for kc in range(2):
                    op("pe", lambda e, pb=pb, oc=oc, kc=kc: e.matmul(pb[:, 0:n], lhsT=pwb[:, l, kc, oc * P:(oc + 1) * P],
                                                                      rhs=cs_[:, kc, 0:n], start=(kc == 0), stop=(kc == 1)),
                       [r_cs, r_const2], [pr], mark=(kc == 1))
                op("act", lambda e, pb=pb, oc=oc: e.activation(out=mixT[:, 4 + oc, 0:n], in_=pb[:, 0:n], func=AF.Copy),
                   [pr], [r_mix[4 + oc]])
            if sample or last_prompt:
                pb, pr = palloc()
                for cc in range(2):
                    op("pe", lambda e, pb=pb, cc=cc: e.transpose(out=pb[:, cc * P:(cc + 1) * P], in_=g32[:, cc, :], identity=ident[:, :]),
                       [r_g32, r_const], [pr], mark=(cc == 1))
                op("dve", lambda e, pb=pb: e.tensor_copy(out=stgo[:, 256:512], in_=pb[:, 0:256]), [pr], [r_stgo[1]])
                if sample:
                    for s_ in range(16):
                        out_rows(1, cs_d[l, s_, 22:30, :], stgo[8 * s_:8 * s_ + 8, 256:512])
                else:
                    out_rows(1, cp_d[l], stgo[98:128, 256:512])

            if ML < 6:
                for _ in range(4):
                    s4_next(); s4_issue()
                return
            for pc in range(2):
                if sample:
                    pv_ = pus[:, pc, :, :]
                    sAv = sA[:, 0:384].rearrange("p (s t) -> p s t", s=16)
                    sBv = sB[:, 0:384].rearrange("p (s t) -> p s t", s=16)
                    W_ = 24
                    sl = lambda v, a, b: v[:, :, a:b]
                    r_p = mx["r_pus"]
                    dv = lambda lo, hi_: dbf[lo:hi_, 0:P].rearrange("p (s t) -> p s t", s=16)
                    hsl = lambda v, lo, hi_, a, b: v[lo:hi_, :, a:b]
                else:
                    pv_ = pu_g[:, pc, :]
                    sAv, sBv = sA, sB
                    W_ = 16 + n
                    sl = lambda v, a, b: v[:, a:b]
                    r_p = r_pu
                    dv = lambda lo, hi_: dbf[lo:hi_, 0:n]
                    hsl = lambda v, lo, hi_, a, b: v[lo:hi_, a:b]

                def tt(out_, in0_, in1_, rd, wr, opx=ALU.add):
                    op("dve", lambda e, out_=out_, in0_=in0_, in1_=in1_, opx=opx: e.tensor_tensor(out=out_, in0=in0_, in1=in1_, op=opx),
                       rd, wr)

                tt(sl(sAv, 1, W_), sl(pv_, 1, W_), sl(pv_, 0, W_ - 1), [r_p], [r_sA])
                tt(sl(sBv, 3, W_), sl(sAv, 3, W_), sl(sAv, 1, W_ - 2), [r_sA], [r_sB])
                if pc == 1:
                    tt(sl(sAv, 7, W_), sl(sBv, 7, W_), sl(sBv, 3, W_ - 4), [r_sB], [r_sA])
                    tt(sl(sBv, 15, W_), sl(sAv, 15, W_), sl(sAv, 7, W_ - 8), [r_sA], [r_sB])
                for (lo, hi_, sv_, rs_) in ((0, 64, sAv, r_sA), (64, 128, sBv, r_sB)):
                    o_, i0_, i1_, sc_ = dv(lo, hi_), hsl(sv_, lo, hi_, 16, W_), hsl(pv_, lo, hi_, 16, W_), invw[lo:hi_, pc:pc + 1]
                    op("dve", lambda e, o_=o_, i0_=i0_, i1_=i1_, sc_=sc_: e.scalar_tensor_tensor(
                        out=o_, in0=i0_, scalar=sc_, in1=i1_, op0=ALU.mult, op1=ALU.subtract), [rs_, r_p, r_const], [r_dbf])
                    if gi == 0:
                        tb, tr = rot("tmp", tmpb, r_tmp)
                        tt(tb[lo:hi_, 0:16], sv_[lo:hi_, 16:32], invcnt[lo:hi_, pc * 16:(pc + 1) * 16], [rs_, r_const], [tr], ALU.mult)
                        tt(dbf[lo:hi_, 0:16], tb[lo:hi_, 0:16], pu_g[lo:hi_, pc, 16:32], [tr, r_p], [r_dbf], ALU.subtract)
                pb, pr = palloc()
                op("pe", lambda e, pb=pb, pc=pc: e.matmul(pb[:, 0:n], lhsT=poolwb[:, l, pc, :], rhs=dbf[:, 0:n], start=True, stop=True),
                   [r_dbf, r_const2], [pr])
                op("act", lambda e, pb=pb, pc=pc: e.activation(out=mixT[:, 6 + pc, 0:n], in_=pb[:, 0:n], func=AF.Copy,
                                                               scale=vT[:, pc, l * 4 + 3:l * 4 + 4]), [pr, r_const], [r_mix[6 + pc]])
            if not sample:
                op("dve", lambda e: e.tensor_copy(out=puc[:, :, :], in_=pu_g[:, :, n:n + 16]), [r_pu], [r_puc])
            if sample or last_prompt:
                pb, pr = palloc()
                for pc in range(2):
                    src = p32[:, pc, :] if sample else pu_g[:, pc, 16 + n - P:16 + n]
                    rsrc = r_p32 if sample else r_pu
                    op("pe", lambda e, pb=pb, pc=pc, src=src: e.transpose(out=pb[:, pc * P:(pc + 1) * P], in_=src, identity=ident[:, :]),
                       [rsrc, r_const], [pr], mark=(pc == 1))
                op("dve", lambda e, pb=pb: e.tensor_copy(out=stgo[:, 512:768], in_=pb[:, 0:256]), [pr], [r_stgo[2]])
                if sample:
                    for s_ in range(16):
                        out_rows(2, ps_d[l, s_, 7:15, :], stgo[8 * s_:8 * s_ + 8, 512:768])
                else:
                    out_rows(2, pp_d[l], stgo[113:128, 512:768])

            if ML < 7:
                for _ in range(4):
                    s4_next(); s4_issue()
                return
            if gi + 1 < NGm and ML >= 7:
                if "pre_st" in mx:
                    mx["pre"] = prenorm_apply_m(l, 2, gi + 1, *mx.pop("pre_st"))
                else:
                    mx["pre"] = prenorm(l, 2, gi + 1)
            A_obuf.recarve()
            r_ob = A_obuf.reg("ob")
            for pi in range(4):
                w, rw = s4_next()
                for ci in range(2):
                    d = 2 * pi + ci
                    pb, pr = palloc()
                    for k in range(KC):
                        op("pe", lambda e, pb=pb, k=k, ci=ci, w=w: e.matmul(
                            pb[:, 0:n], lhsT=w[:, k, ci * P:(ci + 1) * P], rhs=mixT[:, k, 0:n],
                            start=(k == 0), stop=(k == KC - 1)), [rw, r_mix[k]], [pr], mark=(k == KC - 1))
                    proj_out_evac(pb, pr, d, n, r_ob)
                s4_issue()
            postnorm(l, 3, gi, r_ob, half=False)

        pending_post = []

        def flush_post():
            while pending_post:
                pending_post.pop(0)()

        def ffn_phase(l, which, carry):
            ipre = 0 if which == 1 else 4
            r_xn = carry if carry is not None else f_prenorm(l, ipre, 0)
            for gi in range(NFG):
                if gi + 1 < NFG:
                    nxt = (l, ipre, gi + 1)
                elif which == 2 and l + 1 < L and 'f1' in PH:
                    nxt = (l + 1, 0, 0)
                else:
                    nxt = None
                r_xn = ffn(l, which, gi, r_xn, nxt)
            return r_xn

        carry = None
        for l in range(L):
            if 'f1' in PH:
                carry = ffn_phase(l, 1, carry)
                flush_post()
            if 'mx' in PH:
                mixer_begin(l)
                for gi in range(NG):
                    mixer(l, gi)
            if 'f1' in PH and 'mx' in PH:
                for _ in range(4):
                    s11_issue()
            if 'f2' in PH:
                carry = ffn_phase(l, 2, None)
                if carry is None:
                    flush_post()
        flush_post()

        A_big.recarve()
        yst = [big[:, i * 2048:(i + 1) * 2048].bitcast(F32) for i in range(2)]
        r_yst = [A_big.reg("yst0"), A_big.reg("yst1")]
        for t in range(NT // P):
            i = t % 2
            gi = min(t // 4, 4)
            for hb in range(2):
                pb, pr = palloc()
                for kk in range(4):
                    k = hb * 4 + kk
                    op("pe", lambda e, pb=pb, kk=kk, k=k, t=t: e.transpose(out=pb[:, kk * P:(kk + 1) * P],
                                                                         in_=xT[:, k, t * P:(t + 1) * P], identity=ident[:, :]),
                       xregs(gi) + [r_const], [pr], mark=(kk == 3))
                if hb == 0:
                    op("act", lambda e, pb=pb, i=i: e.activation(out=yst[i][:, 0:512], in_=pb[:, :], func=AF.Copy), [pr], [r_yst[i]])
                else:
                    op("dve", lambda e, pb=pb, i=i: e.tensor_copy(out=yst[i][:, 512:1024], in_=pb[:, :]), [pr], [r_yst[i]])
            dma("sp", y_d[t * P:(t + 1) * P, :], yst[i], [r_yst[i]], [], d_y[i])

        for dsem in d_stgo + d_y + [d_cp]:
            if dsem.count:
                T.wait_tok("sp", (dsem, dsem.count))

        block = es.enter_context(nc.Block())
        T.replay(block)
    return nc


_CACHE = {}


def _perm_q():
    idx = []
    for c in range(4):
        for h in (c, c + 4):
            idx.extend(range(h * 64, (h + 1) * 64))
    return np.array(idx)


def prep_inputs(inp, DEPTH):
    L = DEPTH
    f32 = lambda a: np.ascontiguousarray(np.asarray(a, dtype=np.float32))
    pq = _perm_q()
    cols = list(pq) + list(range(512, 768)) + list(range(768, 896)) + list(range(1024, 1152)) + \
        list(range(896, 1024)) + list(range(1152, 1280)) + list(range(1280, 1536))
    cols = np.array(cols)
    w_in = f32(inp["w_in"])[:L][:, :, cols]
    rows = np.concatenate([pq, np.arange(512, 1024)])
    w_out = f32(inp["w_out"])[:L][:, rows, :]
    shared = {
        "ng": f32(inp["norm_g"])[:L].reshape(L * 48, 128),
        "f1g": f32(inp["ffn1_wg"])[:L], "f1u": f32(inp["ffn1_wu"])[:L], "f1d": f32(inp["ffn1_wd"])[:L],
        "win": np.ascontiguousarray(w_in), "wout": np.ascontiguousarray(w_out),
        "f2g": f32(inp["ffn2_wg"])[:L], "f2u": f32(inp["ffn2_wu"])[:L], "f2d": f32(inp["ffn2_wd"])[:L],
        "sinks": f32(inp["attn_sinks"])[:L].reshape(1, L * 8),
        "dww": f32(inp["conv_dw_w"])[:L].reshape(L * 31, 256),
        "vecs": np.ascontiguousarray(np.stack([f32(inp["conv_dw_b"])[:L], f32(inp["conv_ln_g"])[:L],
                                               f32(inp["conv_ln_b"])[:L], f32(inp["pool_scale"])[:L]], axis=1).reshape(L * 4, 256)),
        "pww": f32(inp["conv_pw_w"])[:L],
        "poolw": f32(inp["pool_w"])[:L].reshape(L, 256, 64),
    }
    shared.update(make_consts())
    xp, xs = f32(inp["x_prompt"]), f32(inp["x_sample"])
    sk, sv = f32(inp["state_attn_k"])[:L], f32(inp["state_attn_v"])[:L]
    sc, sp = f32(inp["state_conv"])[:L], f32(inp["state_pool"])[:L]
    maps = []
    for c in range(NCORES):
        m = dict(shared)
        m["x"] = np.ascontiguousarray(np.concatenate([xp[c], xs[16 * c:16 * c + 16].reshape(128, D)], axis=0))
        m["sk"] = np.ascontiguousarray(sk[:, 16 * c:16 * c + 16].reshape(L, 16, 128, 128))
        m["sv"] = np.ascontiguousarray(sv[:, 16 * c:16 * c + 16].reshape(L, 16, 128, 128))
        m["sc"] = np.ascontiguousarray(sc[:, 16 * c:16 * c + 16])
        m["sp"] = np.ascontiguousarray(sp[:, 16 * c:16 * c + 16])
        maps.append(m)
    return maps


def assemble(results, DEPTH):
    L = DEPTH
    R = results
    y = np.stack([r["y"] for r in R])
    y_prompt = np.ascontiguousarray(y[:, :NPR, :])
    y_sample = np.ascontiguousarray(y[:, NPR:, :].reshape(128, 8, D))
    kp = np.stack([r["kp"] for r in R], axis=1).reshape(L, 8, 128, 2, 64)
    vp = np.stack([r["vp"] for r in R], axis=1).reshape(L, 8, 128, 2, 64)
    cp = np.stack([r["cp"] for r in R], axis=1)
    pp = np.stack([r["pp"] for r in R], axis=1)
    ks = np.concatenate([r["ks"] for r in R], axis=1).reshape(L, 128, 128, 2, 64)
    vs = np.concatenate([r["vs"] for r in R], axis=1).reshape(L, 128, 128, 2, 64)
    cs = np.concatenate([r["cs"] for r in R], axis=1)
    ps = np.concatenate([r["ps"] for r in R], axis=1)
    outs = (y_prompt, y_sample, kp, vp, cp, pp, ks, vs, cs, ps)
    return tuple(np.ascontiguousarray(o, dtype=np.float32) for o in outs)


def run(inp, DEPTH, trace=False, **bk):
    key = (DEPTH, tuple(sorted(bk.items())))
    if key not in _CACHE:
        _CACHE[key] = build_program(DEPTH, **bk)
    nc = _CACHE[key]
    maps = prep_inputs(inp, DEPTH)
    res = run_bass_kernel_spmd(nc, maps, core_ids=list(range(NCORES)), **({"trace": True} if trace else {}))
    return assemble(res.results, DEPTH), res


def kernel(**inputs):
    outs, _ = run(inputs, 4)
    return outs
```

```python
import numpy as np
from contextlib import ExitStack
import concourse.bass as bass
import concourse.mybir as mybir
from concourse.bass_utils import run_bass_kernel_spmd
import ml_dtypes

F32 = mybir.dt.float32
BF16 = mybir.dt.bfloat16
AF = mybir.ActivationFunctionType
ALU = mybir.AluOpType

P = 128
D = 1024
KC = 8
DFF = 2816
FC = 22
NPR = 2048
NS = 128
NT = NPR + NS
GROUPS = [(0, 512), (512, 512), (1024, 512), (1536, 512), (2048, 128)]
RMS_EPS = 1e-6
LN_EPS = 1e-5
NEG = -1e30
NCORES = 8
FN = 544
FBLK = [(0, 272), (272, 272)]
NFG = 4


class SemO:
    def __init__(self, handle):
        self.handle = handle
        self.count = 0


class Reg:
    def __init__(self, name, pending=None):
        self.name = name
        self.lw = None
        self.rd = dict(pending) if pending else {}


class Eng:
    def __init__(self, name, sem):
        self.name = name
        self.sem = sem
        self.ops = []
        self.waited = {}


class Arena:
    def __init__(self):
        self.subs = []

    def recarve(self):
        pend = {}
        for r in self.subs:
            if r.lw is not None:
                s, v = r.lw
                pend[s] = max(pend.get(s, 0), v)
            for s, v in r.rd.items():
                pend[s] = max(pend.get(s, 0), v)
        self.subs = []
        self.pend = pend

    def reg(self, name):
        r = Reg(name, self.pend)
        self.subs.append(r)
        return r


class Trk:
    def __init__(self, nc, es):
        self.nc = nc
        self.es = es
        self.engs = {}
        for n in ("pe", "act", "dve", "pool", "sp"):
            self.engs[n] = Eng(n, self.newsem("e_" + n))

    def newsem(self, name):
        return SemO(self.es.enter_context(self.nc.semaphore(name)))

    def _waits(self, E, rd, wr, nowaw=False):
        need = {}

        def add(tok):
            if tok is None:
                return
            s, v = tok
            if need.get(s, 0) < v:
                need[s] = v

        for r in rd:
            add(r.lw)
            if getattr(r, "psum", False):
                for s_, v_ in r.rd.items():
                    if s_ is not E.sem:
                        add((s_, v_))
        for w in wr:
            if not (nowaw and w.lw is not None and w.lw[0] is E.sem):
                add(w.lw)
            for s, v in w.rd.items():
                add((s, v))
        for s, v in need.items():
            if s is E.sem and E.name == "pe":
                continue
            if E.waited.get(s, 0) >= v:
                continue
            E.waited[s] = v
            E.ops.append(("w", s, v))

    def op(self, en, fn, rd=(), wr=(), mark=True, nowaw=False):
        E = self.engs[en]
        self._waits(E, rd, wr, nowaw)
        if mark:
            E.sem.count += 1
            tok = (E.sem, E.sem.count)
        else:
            tok = (E.sem, E.sem.count + 1)
        E.ops.append(("o", fn, mark))
        for r in rd:
            if r.rd.get(E.sem, 0) < tok[1]:
                r.rd[E.sem] = tok[1]
        for w in wr:
            w.lw = tok
            w.rd = {}
        return tok

    def dma(self, qn, out, in_, rd, wr, dsem):
        Q = self.engs[qn]
        self._waits(Q, rd, wr)
        dsem.count += 16
        tok = (dsem, dsem.count)
        Q.ops.append(("d", out, in_, dsem))
        for r in rd:
            if r.rd.get(dsem, 0) < tok[1]:
                r.rd[dsem] = tok[1]
        for w in wr:
            w.lw = tok
            w.rd = {}
        return tok

    def wait_tok(self, en, tok):
        E = self.engs[en]
        s, v = tok
        if E.waited.get(s, 0) < v:
            E.waited[s] = v
            E.ops.append(("w", s, v))

    def replay(self, block):
        nc = self.nc

        def run(E, e):
            for o in E.ops:
                if o[0] == "w":
                    e.wait_ge(o[1].handle, o[2])
                elif o[0] == "o":
                    ins = o[1](e)
                    if o[2]:
                        ins.then_inc(E.sem.handle, 1)
                else:
                    e.dma_start(out=o[1], in_=o[2]).then_inc(o[3].handle, 16)

        engs = self.engs

        @block.tensor
        def _(e):
            run(engs["pe"], e)

        @block.scalar
        def _(e):
            run(engs["act"], e)

        @block.vector
        def _(e):
            run(engs["dve"], e)

        @block.gpsimd
        def _(e):
            run(engs["pool"], e)

        @block.sync
        def _(e):
            run(engs["sp"], e)


def _slopes():
    return np.exp2(-8.0 * np.arange(1, 9, dtype=np.float64) / 8.0)


def make_consts():
    sl = _slopes()
    j = np.arange(128)[:, None]
    i = np.arange(128)[None, :]
    maskp = np.zeros((128, 2, 2, 2, 2, 128), np.float64)
    for c in range(4):
        for hi in range(2):
            h = c + 4 * hi
            dprev = i + 128 - j
            maskp[:, c // 2, hi, c % 2, 0, :] = np.where(j > i, -sl[h] * dprev, NEG)
            dcur = i - j
            maskp[:, c // 2, hi, c % 2, 1, :] = np.where(j <= i, -sl[h] * dcur, NEG)
    msp = np.zeros((128, 2, 16, 4, 8), np.float64)
    t = np.arange(8)[None, :]
    for kvh in range(2):
        for g in range(4):
            h = kvh * 4 + g
            dist = t + 128 - j
            msp[:, kvh, :, g, :] = np.where(j > t, -sl[h] * dist, NEG)[:, None, :]
    msc = np.full((128, 2, 4, 128), NEG, np.float64)
    sp_ = np.arange(128) // 8
    tp_ = np.arange(128) % 8
    same = sp_[:, None] == sp_[None, :]
    dt_ = tp_[None, :] - tp_[:, None]
    ok = same & (dt_ >= 0)
    for c in range(4):
        for hi in range(2):
            h = c + 4 * hi
            msc[:, hi, c, :] = np.where(ok, -sl[h] * dt_, NEG)
    bf = ml_dtypes.bfloat16
    wins = (2, 4, 8, 16)
    invw = np.zeros((128, 2), np.float32)
    invcnt = np.zeros((128, 2, 16), np.float32)
    for pc in range(2):
        for half in range(2):
            w = wins[pc * 2 + half]
            invw[half * 64:(half + 1) * 64, pc] = 1.0 / w
            cnt = np.minimum(w, np.arange(16) + 1).astype(np.float32)
            invcnt[half * 64:(half + 1) * 64, pc, :] = 1.0 / cnt
    return {
        "c_ident": np.eye(128, dtype=np.float32),
        "c_maskp": maskp.reshape(128, 2048).astype(np.float32).astype(bf),
        "c_msp": msp.reshape(128, 1024).astype(np.float32).astype(bf),
        "c_msc": msc.reshape(128, 1024).astype(np.float32).astype(bf),
        "c_invw": invw,
        "c_invcnt": invcnt.reshape(128, 32),
    }


def build_program(DEPTH, PH=('f1', 'mx', 'f2'), CP=True, FL=4, NGRP=5, DBG=(), ML=7):
    nc = bass.Bass("TRN2", target_bir_lowering=False)
    L = DEPTH

    def din(name, shape, dt=F32):
        return nc.dram_tensor(name, list(shape), dt, kind="ExternalInput").ap()

    def dout(name, shape):
        return nc.dram_tensor(name, list(shape), F32, kind="ExternalOutput").ap()

    x_d = din("x", [NT, D])
    sk_d = din("sk", [L, 16, 128, 128])
    sv_d = din("sv", [L, 16, 128, 128])
    sc_d = din("sc", [L, 16, 30, 256])
    sp_d = din("sp", [L, 16, 15, 256])
    ng_d = din("ng", [L * 48, 128])
    wts = {}
    for nm, shp in (("f1g", [L, D, DFF]), ("f1u", [L, D, DFF]), ("f1d", [L, DFF, D]),
                    ("win", [L, D, 1536]), ("wout", [L, D, D]),
                    ("f2g", [L, D, DFF]), ("f2u", [L, D, DFF]), ("f2d", [L, DFF, D])):
        wts[nm] = din(nm, shp)
    sinks_d = din("sinks", [1, L * 8])
    dww_d = din("dww", [L * 31, 256])
    vecs_d = din("vecs", [L * 4, 256])
    pww_d = din("pww", [L, 256, 256])
    poolw_d = din("poolw", [L, 256, 64])
    ident_d = din("c_ident", [128, 128])
    maskp_d = din("c_maskp", [128, 2048], BF16)
    msp_d = din("c_msp", [128, 1024], BF16)
    msc_d = din("c_msc", [128, 1024], BF16)
    invw_d = din("c_invw", [128, 2])
    invcnt_d = din("c_invcnt", [128, 32])

    y_d = dout("y", [NT, D])
    kp_d = dout("kp", [L, 128, 128])
    vp_d = dout("vp", [L, 128, 128])
    cp_d = dout("cp", [L, 30, 256])
    pp_d = dout("pp", [L, 15, 256])
    ks_d = dout("ks", [L, 16, 128, 128])
    vs_d = dout("vs", [L, 16, 128, 128])
    cs_d = dout("cs", [L, 16, 30, 256])
    ps_d = dout("ps", [L, 16, 15, 256])

    es = ExitStack()
    with es:
        T = Trk(nc, es)
        op, dma = T.op, T.dma

        def sb(name, shape, dt):
            return nc.alloc_sbuf_tensor(name, list(shape), dt)


        xT = sb("xT", [P, KC, NT], F32)
        big = sb("big", [P, FC * FN], BF16)
        obuf = sb("obuf", [P, KC * FN], F32)
        xnT = sb("xnT", [P, KC, FN], BF16)
        sq = sb("sq", [P, KC, FN], BF16)
        NS4 = 4
        s4 = [sb(f"s4_{i}", [P, KC, 256], BF16) for i in range(NS4)]
        s11 = [sb(f"s11_{i}", [P, FC * 256], BF16) for i in range(2)]
        rstd = [sb(f"rstd{i}", [P, FN], F32) for i in range(2)]
        sgb = [sb(f"sg{i}", [P, FN], F32) for i in range(2)]
        tmpb = [sb(f"tmp{i}", [P, FN], F32) for i in range(2)]
        dring = [sb(f"dg{i}", [P, 128], BF16) for i in range(8)]
        PTb = [sb(f"PT{i}", [P, 512], BF16) for i in range(4)]
        Rb = [sb(f"R{i}", [P, 128], F32) for i in range(2)]
        stgo = sb("stgo", [P, 768], F32)
        puc = sb("puc", [P, 2, 16], F32)
        ident = sb("ident", [P, 128], F32)
        identb = sb("identb", [P, 128], BF16)
        onesb = sb("onesb", [P, 128], BF16)
        onesm = sb("onesm", [P, 128], BF16)
        ones256 = sb("ones256", [P, 128], BF16)
        ones32 = sb("ones32", [P, 128], F32)
        maskp = sb("maskp", [P, 2048], BF16)
        msp = sb("msp", [P, 1024], BF16)
        msc = sb("msc", [P, 1024], BF16)
        invw = sb("invw", [P, 2], F32)
        invcnt = sb("invcnt", [P, 32], F32)
        gT = sb("gT", [P, L * 48], F32)
        gTh = sb("gTh", [P, L * 48], F32)
        dwT = sb("dwT", [P, 2, L * 31], F32)
        vT = sb("vT", [P, 2, L * 4], F32)
        esk = sb("esk", [P, L * 8], F32)
        esc = sb("esc", [P, L * 4], F32)
        pwb = sb("pwb", [P, L, 2, 256], BF16)
        poolwb = sb("poolwb", [P, L, 2, 128], BF16)
        sinks_sb = sb("sinks_sb", [1, L * 8], F32)
        epsr = sb("epsr", [P, 1], F32)
        negt = sb("negt", [P, 128], BF16)
        epsl = sb("epsl", [P, 1], F32)

        pbank = [nc.alloc_psum_tensor(f"pb{i}", [P, 512], F32) for i in range(8)]
        pregs = [Reg(f"pb{i}") for i in range(8)]
        for r_ in pregs:
            r_.psum = True
        pctr = [0]

        def palloc():
            i = pctr[0] % 8
            pctr[0] += 1
            return pbank[i], pregs[i]

        A_big, A_obuf, A_xn = Arena(), Arena(), Arena()
        A_s11 = [Arena() for _ in range(4)]
        for a in [A_big, A_obuf, A_xn] + A_s11:
            a.recarve()

        def multi_reg(name, arenas):
            pend = {}
            for a in arenas:
                for s_, v_ in a.pend.items():
                    pend[s_] = max(pend.get(s_, 0), v_)
            r = Reg(name, pend)
            for a in arenas:
                a.subs.append(r)
            return r

        NG = NGRP
        r_xT = [Reg(f"xT{g}") for g in range(4)]
        r_xTs = [Reg(f"xTs{g}") for g in range(4)]

        def xregs(gi_):
            return [r_xT[gi_]] if gi_ < 4 else list(r_xTs)
        r_sq = Reg("sq")
        r_rstd = [Reg("rstd0"), Reg("rstd1")]
        r_sg = [Reg("sg0"), Reg("sg1")]
        r_tmp = [Reg("tmp0"), Reg("tmp1")]
        r_PT = [Reg(f"PT{i}") for i in range(4)]
        r_R = [Reg("R0"), Reg("R1")]
        r_const = Reg("const")
        r_puc = Reg("puc")
        r_dring = [Reg(f"dg{i}") for i in range(8)]
        r_stgo = [Reg("stgo0"), Reg("stgo1"), Reg("stgo2")]
        ctr = {"rstd": 0, "sg": 0, "tmp": 0, "PT": 0, "R": 0, "dg": 0}

        def rot(kind, bufs, regs):
            i = ctr[kind] % len(bufs)
            ctr[kind] += 1
            return bufs[i], regs[i]

        d_c1 = T.newsem("d_c1")
        d_c2 = T.newsem("d_c2")
        d_ld = T.newsem("d_ld")
        d_stgo = [T.newsem(f"d_stgo{i}") for i in range(3)]
        d_y = [T.newsem(f"d_y{i}") for i in range(2)]
        d_s4 = [T.newsem(f"d_s4_{i}") for i in range(NS4)]
        d_s11 = [T.newsem(f"d_s11_{i}") for i in range(4)]
        d_xin = [T.newsem(f"d_xin{i}") for i in range(2)]
        d_stK = [T.newsem("d_stK0"), T.newsem("d_stK1")]
        d_stC = T.newsem("d_stC")
        d_vs = T.newsem("d_vs")
        d_cp = T.newsem("d_cp")
        r_s4 = [Reg(f"s4_{i}") for i in range(NS4)]

        s4_list, s11_list = [], []
        for l in range(L):
            for gi in range(NFG if 'f1' in PH else 0):
                for fp in range(11):
                    s4_list.append(wts["f1g"][l, :, fp * 256:(fp + 1) * 256])
                    s4_list.append(wts["f1u"][l, :, fp * 256:(fp + 1) * 256])
                for dp in range(4):
                    for hh in range(2):
                        s11_list.append(wts["f1d"][l, hh * 1408:(hh + 1) * 1408, dp * 256:(dp + 1) * 256])
            for gi in range(NG if 'mx' in PH else 0):
                for pi in range(6):
                    s4_list.append(wts["win"][l, :, pi * 256:(pi + 1) * 256])
                for pi in range(4):
                    s4_list.append(wts["wout"][l, :, pi * 256:(pi + 1) * 256])
            for gi in range(NFG if 'f2' in PH else 0):
                for fp in range(11):
                    s4_list.append(wts["f2g"][l, :, fp * 256:(fp + 1) * 256])
                    s4_list.append(wts["f2u"][l, :, fp * 256:(fp + 1) * 256])
                for dp in range(4):
                    for hh in range(2):
                        s11_list.append(wts["f2d"][l, hh * 1408:(hh + 1) * 1408, dp * 256:(dp + 1) * 256])
        s4_state = {"issued": 0, "next": 0}
        s11_state = {"issued": 0, "next": 0}
        s11_regs = [None] * 4

        def s4_issue():
            j = s4_state["issued"]
            if j >= len(s4_list):
                return
            s = j % NS4
            src = s4_list[j].rearrange("(k p) n -> p k n", p=P)
            dma("pool", s4[s][:, :, :], src, [], [r_s4[s]], d_s4[s])
            s4_state["issued"] = j + 1

        def s4_next():
            j = s4_state["next"]
            assert j < s4_state["issued"]
            s4_state["next"] = j + 1
            s = j % NS4
            return s4[s], r_s4[s]

        def s11_ap(hs):
            t_, h_ = hs // 2, hs % 2
            return s11[t_][:, h_ * 2816:(h_ + 1) * 2816].rearrange("p (k n) -> p k n", k=11)

        def s11_issue():
            j = s11_state["issued"]
            if j >= len(s11_list):
                return
            hs = j % 4
            ar = A_s11[hs]
            ar.recarve()
            r = ar.reg(f"s11w{hs}")
            s11_regs[hs] = r
            src = s11_list[j].rearrange("(k p) n -> p k n", p=P)
            dma("pool", s11_ap(hs), src, [], [r], d_s11[hs])
            s11_state["issued"] = j + 1

        def s11_next():
            j = s11_state["next"]
            assert j < s11_state["issued"], (j, s11_state)
            s11_state["next"] = j + 1
            hs = j % 4
            return s11_ap(hs), s11_regs[hs]

        for dst, src in ((ident[:, :], ident_d[:, :]), (maskp[:, :], maskp_d[:, :]), (msp[:, :], msp_d[:, :]),
                         (msc[:, :], msc_d[:, :]), (invw[:, :], invw_d[:, :]), (invcnt[:, :], invcnt_d[:, :]),
                         (sinks_sb[:, :], sinks_d[:, :])):
            dma("sp", dst, src, [], [r_const], d_c1)
        op("dve", lambda e: e.memset(onesb[:, :], 1.0), [], [r_const])
        op("dve", lambda e: e.memset(onesm[:, :], 1.0 / 1024.0), [], [r_const])
        op("dve", lambda e: e.memset(ones256[:, :], 1.0 / 256.0), [], [r_const])
        op("dve", lambda e: e.memset(ones32[:, :], 1.0), [], [r_const])
        op("dve", lambda e: e.memset(epsr[:, :], RMS_EPS), [], [r_const])
        op("dve", lambda e: e.memset(negt[:, :], NEG), [], [r_const])
        op("dve", lambda e: e.memset(epsl[:, :], LN_EPS), [], [r_const])
        op("dve", lambda e: e.memset(poolwb[:, :, :, :], 0.0), [], [r_const])
        op("dve", lambda e: e.tensor_copy(out=identb[:, :], in_=ident[:, :]), [r_const], [r_const])
        r_const2 = Reg("const2")
        T.wait_tok("pool", r_const.lw)
        for l in range(L):
            dma("pool", pwb[:, l, :, :], pww_d[l].rearrange("(k p) n -> p k n", p=P), [], [r_const2], d_c2)
            for g in range(4):
                pc, half = g // 2, g % 2
                dma("pool", poolwb[half * 64:(half + 1) * 64, l, pc, half * 64:(half + 1) * 64],
                    poolw_d[l, g * 64:(g + 1) * 64, :], [], [r_const2], d_c2)

        for l in range(L if CP else 0):
            dma("sp", ks_d[l, :, 0:120, :], sk_d[l, :, 8:128, :], [], [], d_cp)
            dma("sp", vs_d[l, :, 0:120, :], sv_d[l, :, 8:128, :], [], [], d_cp)
            dma("sp", cs_d[l, :, 0:22, :], sc_d[l, :, 8:30, :], [], [], d_cp)
            dma("sp", ps_d[l, :, 0:7, :], sp_d[l, :, 8:15, :], [], [], d_cp)

        for _ in range(NS4):
            s4_issue()
        for _ in range(4):
            s11_issue()

        r_stg = A_obuf.reg("stg")

        def load_T(src_rows, nrows, nchunks, dst_fn):
            dma("sp", obuf[0:nrows, 0:nchunks * 128], src_rows, [], [r_stg], d_ld)
            for cc in range(nchunks):
                pb, pr = palloc()
                op("pe", lambda e, cc=cc, pb=pb: e.transpose(out=pb[:, 0:nrows], in_=obuf[0:nrows, cc * 128:(cc + 1) * 128],
                                                              identity=ident[0:nrows, 0:nrows]),
                   [r_stg, r_const], [pr])
                op("dve", lambda e, cc=cc, pb=pb: e.tensor_copy(out=dst_fn(cc), in_=pb[:, 0:nrows]), [pr], [r_const])

        nrow_g = L * 48
        r0 = 0
        while r0 < nrow_g:
            nr = min(96, nrow_g - r0)
            load_T(ng_d[r0:r0 + nr, :], nr, 1, lambda cc, r0=r0, nr=nr: gT[:, r0:r0 + nr])
            r0 += nr
        op("dve", lambda e: e.tensor_scalar(out=gTh[:, :], in0=gT[:, :], scalar1=0.5, scalar2=None, op0=ALU.mult),
           [r_const], [r_const])
        load_T(dww_d[:, :], L * 31, 2, lambda cc: dwT[:, cc, :])
        load_T(vecs_d[:, :], L * 4, 2, lambda cc: vT[:, cc, :])
        pb, pr = palloc()
        op("pe", lambda e, pb=pb: e.matmul(pb[:, 0:L * 8], lhsT=ones32[0:1, :], rhs=sinks_sb[0:1, :], start=True, stop=True),
           [r_const], [pr])
        op("act", lambda e, pb=pb: e.activation(out=esk[:, :], in_=pb[:, 0:L * 8], func=AF.Exp), [pr], [r_const])
        eskv = esk[:, :].rearrange("p (l h) -> p l h", h=8)
        op("dve", lambda e: e.tensor_copy(out=esc[0:64, :].rearrange("p (l c) -> p l c", c=4), in_=eskv[0:64, :, 0:4]),
           [r_const], [r_const])
        op("dve", lambda e: e.tensor_copy(out=esc[64:128, :].rearrange("p (l c) -> p l c", c=4), in_=eskv[64:128, :, 4:8]),
           [r_const], [r_const])

        xin = [big[:, i * 2048:(i + 1) * 2048].bitcast(F32) for i in range(2)]
        r_xin = [A_big.reg("xin0"), A_big.reg("xin1")]
        for t in range(NT // P):
            i = t % 2
            gi = min(t // 4, 4)
            dma("sp", xin[i], x_d[t * P:(t + 1) * P, :], [], [r_xin[i]], d_xin[i])
            for hb in range(2):
                pb, pr = palloc()
                for kk in range(4):
                    k = hb * 4 + kk
                    op("pe", lambda e, pb=pb, kk=kk, k=k, i=i: e.transpose(out=pb[:, kk * P:(kk + 1) * P],
                                                                         in_=xin[i][:, k * P:(k + 1) * P], identity=ident[:, :]),
                       [r_xin[i], r_const], [pr], mark=(kk == 3))
                if hb == 0:
                    op("act", lambda e, pb=pb, hb=hb, t=t: e.activation(
                        out=xT[:, hb * 4:hb * 4 + 4, t * P:(t + 1) * P],
                        in_=pb[:, :].rearrange("p (a b) -> p a b", a=4), func=AF.Copy), [pr], xregs(gi))
                else:
                    op("dve", lambda e, pb=pb, hb=hb, t=t: e.tensor_copy(
                        out=xT[:, hb * 4:hb * 4 + 4, t * P:(t + 1) * P],
                        in_=pb[:, :].rearrange("p (a b) -> p a b", a=4)), [pr], xregs(gi))

        def norm_stats(src_reg, n, src=None):
            src = sq if src is None else src
            pb, pr = palloc()
            for k in range(KC):
                op("pe", lambda e, pb=pb, k=k: e.matmul(pb[:, 0:n], lhsT=onesm[:, :], rhs=src[:, k, 0:n],
                                                        start=(k == 0), stop=(k == KC - 1)),
                   [src_reg, r_const], [pr], mark=(k == KC - 1))
            rs, rr = rot("rstd", rstd, r_rstd)
            op("act", lambda e, pb=pb, rs=rs: e.activation(out=rs[:, 0:n], in_=pb[:, 0:n], func=AF.Sqrt, bias=epsr[:, 0:1]),
               [pr, r_const], [rr])
            op("dve", lambda e, rs=rs: e.reciprocal(out=rs[:, 0:n], in_=rs[:, 0:n]), [rr], [rr])
            return rs, rr

        cur = {}

        def prenorm(l, i, gi):
            g0, n = GROUPS[gi]
            A_xn.recarve()
            r_xsq = A_xn.reg("xsq")
            op("act", lambda e: e.activation(out=xnT[:, :, 0:n], in_=xT[:, :, g0:g0 + n], func=AF.Square),
               xregs(gi), [r_xsq])
            rs, rr = norm_stats(r_xsq, n, src=xnT)
            A_xn.recarve()
            r_xn = A_xn.reg("xn")
            cur["r_xn"] = r_xn
            for k in range(KC):
                col = l * 48 + i * 8 + k
                op("dve", lambda e, k=k, col=col, rs=rs: e.scalar_tensor_tensor(
                    out=xnT[:, k, 0:n], in0=xT[:, k, g0:g0 + n], scalar=gT[:, col:col + 1], in1=rs[:, 0:n],
                    op0=ALU.mult, op1=ALU.mult), xregs(gi) + [rr, r_const], [r_xn], nowaw=True)
            return r_xn

        ob3 = obuf[:, :].rearrange("p (k n) -> p k n", k=KC)

        def prenorm_stats_m(l, i, gi):
            g0, n = GROUPS[gi]
            op("act", lambda e: e.activation(out=sq[:, :, 0:n], in_=xT[:, :, g0:g0 + n], func=AF.Square), xregs(gi), [r_sq])
            return norm_stats(r_sq, n)

        def prenorm_apply_m(l, i, gi, rs, rr):
            g0, n = GROUPS[gi]
            A_xn.recarve()
            r_xn = A_xn.reg("xn")
            for k in range(KC):
                col = l * 48 + i * 8 + k
                op("dve", lambda e, k=k, col=col, rs=rs: e.scalar_tensor_tensor(
                    out=xnT[:, k, 0:n], in0=xT[:, k, g0:g0 + n], scalar=gT[:, col:col + 1], in1=rs[:, 0:n],
                    op0=ALU.mult, op1=ALU.mult), xregs(gi) + [rr, r_const], [r_xn], nowaw=True)
            return r_xn

        def postnorm(l, i, gi, r_ob, half):
            g0, n = GROUPS[gi]
            rs, rr = norm_stats(r_sq, n)
            gsrc = gTh if half else gT
            for d0 in range(0, KC, 2):
                tbs = []
                for d in (d0, d0 + 1):
                    col = l * 48 + i * 8 + d
                    tb, tr = rot("tmp", tmpb, r_tmp)
                    tbs.append((d, tb, tr))
                    op("dve", lambda e, d=d, col=col, tb=tb, rs=rs: e.scalar_tensor_tensor(
                        out=tb[:, 0:n], in0=ob3[:, d, 0:n], scalar=gsrc[:, col:col + 1], in1=rs[:, 0:n],
                        op0=ALU.mult, op1=ALU.mult), [r_ob, rr, r_const], [tr])
                for (d, tb, tr) in tbs:
                    op("dve", lambda e, d=d, tb=tb: e.tensor_tensor(
                        out=xT[:, d, g0:g0 + n], in0=xT[:, d, g0:g0 + n], in1=tb[:, 0:n], op=ALU.add),
                       [tr] + xregs(gi), xregs(gi), nowaw=True)

        def proj_out_evac(pb, pr, d, n, r_ob):
            op("dve", lambda e, pb=pb, d=d: e.tensor_copy(out=ob3[:, d, 0:n], in_=pb[:, 0:n]), [pr], [r_ob], nowaw=True)
            op("act", lambda e, d=d: e.activation(out=sq[:, d, 0:n], in_=ob3[:, d, 0:n], func=AF.Square), [r_ob], [r_sq], nowaw=True)

        def mm8(pb_ap, pr, w, rw, col0, ncol, rhs_fn, rhs_regs, n=None):
            for k in range(KC):
                op("pe", lambda e, k=k: e.matmul(pb_ap, lhsT=w[:, k, col0:col0 + ncol], rhs=rhs_fn(k),
                                                 start=(k == 0), stop=(k == KC - 1)),
                   [rw] + rhs_regs, [pr], mark=(k == KC - 1))

        def fsegs(g):
            return [(g * 512, 512, 0, r_xT[g]), (NPR + 32 * g, 32, 512, r_xTs[g])]

        def f_stats(src, src_reg):
            rs, rr = rot("rstd", rstd, r_rstd)
            for (b0, bn) in FBLK:
                pb, pr = palloc()
                for k in range(KC):
                    op("pe", lambda e, pb=pb, k=k, b0=b0, bn=bn: e.matmul(pb[:, 0:bn], lhsT=onesm[:, :], rhs=src[:, k, b0:b0 + bn],
                                                                          start=(k == 0), stop=(k == KC - 1)),
                       [src_reg, r_const], [pr], mark=(k == KC - 1))
                op("act", lambda e, pb=pb, rs=rs, b0=b0, bn=bn: e.activation(out=rs[:, b0:b0 + bn], in_=pb[:, 0:bn], func=AF.Sqrt,
                                                                            bias=epsr[:, 0:1]), [pr, r_const], [rr])
            op("dve", lambda e, rs=rs: e.reciprocal(out=rs[:, 0:FN], in_=rs[:, 0:FN]), [rr], [rr])
            return rs, rr

        def f_prenorm_stats(l, i, g):
            for (c0, n, o, rx) in fsegs(g):
                op("act", lambda e, c0=c0, n=n, o=o: e.activation(out=sq[:, :, o:o + n], in_=xT[:, :, c0:c0 + n], func=AF.Square),
                   [rx], [r_sq])
            return f_stats(sq, r_sq)

        def f_prenorm_apply(l, i, g, rs, rr):
            A_xn.recarve()
            r_xn = A_xn.reg("xn")
            for k in range(KC):
                col = l * 48 + i * 8 + k
                for (c0, n, o, rx) in fsegs(g):
                    op("dve", lambda e, k=k, col=col, rs=rs, c0=c0, n=n, o=o: e.scalar_tensor_tensor(
                        out=xnT[:, k, o:o + n], in0=xT[:, k, c0:c0 + n], scalar=gT[:, col:col + 1], in1=rs[:, o:o + n],
                        op0=ALU.mult, op1=ALU.mult), [rx, rr, r_const], [r_xn], nowaw=True)
            return r_xn

        def f_prenorm(l, i, g):
            rs, rr = f_prenorm_stats(l, i, g)
            return f_prenorm_apply(l, i, g, rs, rr)

        def ffn(l, which, g, r_xn, next_pre=None):
            ipre = 0 if which == 1 else 4
            A_big.recarve()
            r_hT = [A_big.reg(f"hT{f}") for f in range(FC)]
            hT = big[:, 0:FC * FN].rearrange("p (f n) -> p f n", f=FC)
            for fp in range(11):
                wg, rg = s4_next()
                wu, ru = s4_next()
                for fi in range(2):
                    f = 2 * fp + fi
                    for (b0, bn) in FBLK:
                        pg, prg = palloc()
                        pu, pru = palloc()
                        mm8(pg[:, 0:bn], prg, wg, rg, fi * P, P, lambda k, b0=b0, bn=bn: xnT[:, k, b0:b0 + bn], [r_xn])
                        mm8(pu[:, 0:bn], pru, wu, ru, fi * P, P, lambda k, b0=b0, bn=bn: xnT[:, k, b0:b0 + bn], [r_xn])
                        sgt, sgr = rot("sg", sgb, r_sg)
                        op("act", lambda e, pg=pg, sgt=sgt, bn=bn: e.activation(out=sgt[:, 0:bn], in_=pg[:, 0:bn], func=AF.Silu),
                           [prg], [sgr])
                        op("dve", lambda e, pu=pu, sgt=sgt, f=f, b0=b0, bn=bn: e.tensor_tensor(
                            out=hT[:, f, b0:b0 + bn], in0=sgt[:, 0:bn], in1=pu[:, 0:bn], op=ALU.mult), [pru, sgr], [r_hT[f]], nowaw=True)
                s4_issue()
                s4_issue()
                if fp == 5 and next_pre is not None:
                    nst = f_prenorm_stats(*next_pre)
                if pending_post:
                    pending_post.pop(0)()
            nxt = f_prenorm_apply(*next_pre, *nst) if next_pre is not None else None
            A_obuf.recarve()
            r_ob = A_obuf.reg("ob")
            for dp in range(4):
                skip = (which == 1 and g == NFG - 1 and dp >= 2 and 'mx' in PH)
                acc = {}
                for di in range(2):
                    for bi in range(2):
                        acc[(di, bi)] = palloc()
                for hh in range(2):
                    wd, rd_ = s11_next()
                    for di in range(2):
                        for bi, (b0, bn) in enumerate(FBLK):
                            pb, pr = acc[(di, bi)]
                            for fl in range(11):
                                f = hh * 11 + fl
                                op("pe", lambda e, pb=pb, f=f, fl=fl, di=di, wd=wd, b0=b0, bn=bn: e.matmul(
                                    pb[:, 0:bn], lhsT=wd[:, fl, di * P:(di + 1) * P], rhs=hT[:, f, b0:b0 + bn],
                                    start=(f == 0), stop=(f == FC - 1)), [rd_, r_hT[f]], [pr], mark=(fl == 10))
                    if not skip:
                        s11_issue()
                for di in range(2):
                    d = 2 * dp + di
                    for bi, (b0, bn) in enumerate(FBLK):
                        pb, pr = acc[(di, bi)]
                        op("dve", lambda e, pb=pb, d=d, b0=b0, bn=bn: e.tensor_copy(out=ob3[:, d, b0:b0 + bn], in_=pb[:, 0:bn]), [pr], [r_ob], nowaw=True)
                        op("act", lambda e, d=d, b0=b0, bn=bn: e.activation(out=sq[:, d, b0:b0 + bn], in_=ob3[:, d, b0:b0 + bn], func=AF.Square),
                           [r_ob], [r_sq], nowaw=True)
            rs, rr = f_stats(sq, r_sq)

            def post_chunk(d):
                col = l * 48 + (ipre + 1) * 8 + d
                tb, tr = rot("tmp", tmpb, r_tmp)
                op("dve", lambda e, d=d, col=col, tb=tb, rs=rs: e.scalar_tensor_tensor(
                    out=tb[:, 0:FN], in0=ob3[:, d, 0:FN], scalar=gTh[:, col:col + 1], in1=rs[:, 0:FN],
                    op0=ALU.mult, op1=ALU.mult), [r_ob, rr, r_const], [tr])
                for (c0, n, o, rx) in fsegs(g):
                    op("dve", lambda e, d=d, tb=tb, c0=c0, n=n, o=o: e.tensor_tensor(
                        out=xT[:, d, c0:c0 + n], in0=xT[:, d, c0:c0 + n], in1=tb[:, o:o + n], op=ALU.add),
                       [tr, rx], [rx], nowaw=True)

            for d in range(KC):
                pending_post.append(lambda d=d: post_chunk(d))
            return nxt

        mx = {}

        def mixer_begin(l):
            for a_ in A_s11:
                a_.recarve()
            A_big.recarve()
            A_obuf.recarve()
            mx["kT"] = s11[0][:, 0:2304]
            mx["r_kT"] = multi_reg("kT", A_s11[0:2])
            mx["Vtm"] = s11[0][:, 2304:4608].rearrange("p (t c) -> p t c", t=18)
            mx["r_V"] = multi_reg("Vtm", A_s11[0:2])
            mx["KsT"] = s11[1][:, 0:2048].rearrange("p (s k) -> p s k", s=16)
            mx["r_KsT"] = multi_reg("KsT", A_s11[2:4])
            mx["Vs"] = s11[1][:, 2048:4096].rearrange("p (s c) -> p s c", s=16)
            mx["r_Vs"] = multi_reg("Vs", A_s11[2:4])
            mx["pus"] = s11[1][:, 4096:5632].bitcast(F32).rearrange("p (c s t) -> p c s t", c=2, s=16)
            mx["r_pus"] = multi_reg("pus", A_s11[2:4])
            mx["qT"] = big[:, 0:2048].rearrange("p (c n) -> p c n", c=4)
            mx["r_qT"] = A_big.reg("qT")
            mx["mixT"] = big[:, 2048:6144].rearrange("p (c n) -> p c n", c=8)
            mx["r_mix"] = [A_big.reg(f"mix{c}") for c in range(8)]
            mx["gx"] = big[:, 6144:7228].rearrange("p (c n) -> p c n", c=2)
            mx["r_gx"] = A_big.reg("gx")
            mx["PTp"] = big[:, 7228:8252]
            mx["r_PTp"] = A_big.reg("PTp")
            mx["PTc"] = big[:, 8252:9276]
            mx["r_PTc"] = A_big.reg("PTc")
            mx["gxs"] = big[:, 9276:10492].rearrange("p (c s t) -> p c s t", c=2, s=16)
            mx["r_gxs"] = A_big.reg("gxs")
            kT, Vtm, gx = mx["kT"], mx["Vtm"], mx["gx"]
            op("dve", lambda e: e.memset(kT[:, 0:128], 0.0), [], [mx["r_kT"]])
            op("dve", lambda e: e.memset(Vtm[:, 0, :], 0.0), [], [mx["r_V"]])
            op("dve", lambda e: e.memset(gx[:, :, 0:30], 0.0), [], [mx["r_gx"]])
            op("dve", lambda e: e.memset(puc[:, :, :], 0.0), [], [r_puc])
            op("dve", lambda e: e.memset(mx["pus"][:, :, :, 0:1], 0.0), [], [mx["r_pus"]])
            if 'nosprep' in DBG:
                return
            Vs, KsT, pus, gxs = mx["Vs"], mx["KsT"], mx["pus"], mx["gxs"]
            for h in range(2):
                dma("pool", Vs[:, 8 * h:8 * h + 8, :], sv_d[l, 8 * h:8 * h + 8].rearrange("s k c -> k s c"), [], [mx["r_Vs"]], d_vs)
            stK = [obuf[:, h * 1024:(h + 1) * 1024].rearrange("p (s c) -> p s c", s=8) for h in range(2)]
            r_stK = [A_obuf.reg("stK0"), A_obuf.reg("stK1")]
            for h in range(2):
                dma("sp", stK[h], sk_d[l, 8 * h:8 * h + 8].rearrange("s k c -> k s c"), [], [r_stK[h]], d_stK[h])
                for q4 in range(2):
                    pb, pr = palloc()
                    for ss in range(4):
                        s_ = q4 * 4 + ss
                        op("pe", lambda e, pb=pb, ss=ss, s_=s_, h=h: e.transpose(
                            out=pb[:, ss * P:(ss + 1) * P], in_=stK[h][:, s_, :], identity=ident[:, :]),
                           [r_stK[h], r_const], [pr], mark=(ss == 3))
                    s0 = h * 8 + q4 * 4
                    op("act", lambda e, pb=pb, s0=s0: e.activation(
                        out=KsT[:, s0:s0 + 4, :], in_=pb[:, :].rearrange("p (a b) -> p a b", a=4), func=AF.Copy),
                       [pr], [mx["r_KsT"]])
            stC = obuf[:, 2048:2304]
            r_stC = A_obuf.reg("stC")
            for i4 in range(4):
                dma("sp", stC[0:120, :], sc_d[l, 4 * i4:4 * i4 + 4].rearrange("s r c -> (s r) c"), [], [r_stC], d_stC)
                for cc in range(2):
                    pb, pr = palloc()
                    op("pe", lambda e, pb=pb, cc=cc: e.transpose(out=pb[:, 0:120], in_=stC[0:120, cc * P:(cc + 1) * P],
                                                                 identity=ident[0:120, 0:120]), [r_stC, r_const], [pr])
                    op("dve", lambda e, pb=pb, cc=cc, i4=i4: e.tensor_copy(
                        out=gxs[:, cc, 4 * i4:4 * i4 + 4, 0:30], in_=pb[:, 0:120].rearrange("p (s r) -> p s r", s=4)),
                       [pr], [mx["r_gxs"]])
            for i8 in range(2):
                dma("sp", stC[0:120, :], sp_d[l, 8 * i8:8 * i8 + 8].rearrange("s r c -> (s r) c"), [], [r_stC], d_stC)
                for pc in range(2):
                    pb, pr = palloc()
                    op("pe", lambda e, pb=pb, pc=pc: e.transpose(out=pb[:, 0:120], in_=stC[0:120, pc * P:(pc + 1) * P],
                                                                 identity=ident[0:120, 0:120]), [r_stC, r_const], [pr])
                    op("dve", lambda e, pb=pb, pc=pc, i8=i8: e.tensor_copy(
                        out=pus[:, pc, 8 * i8:8 * i8 + 8, 1:16], in_=pb[:, 0:120].rearrange("p (s r) -> p s r", s=8)),
                       [pr], [mx["r_pus"]])

        def out_rows(si, dst_ap, src_ap, extra_rd=()):
            dma("sp", dst_ap, src_ap, [r_stgo[si]] + list(extra_rd), [], d_stgo[si])

        def mixer(l, gi):
            g0, n = GROUPS[gi]
            NGm = NG
            sample = (gi == 4)
            last_prompt = (gi == 3)
            kT, Vtm, qT, mixT, gx = mx["kT"], mx["Vtm"], mx["qT"], mx["mixT"], mx["gx"]
            r_kT, r_V, r_qT, r_mix, r_gx = mx["r_kT"], mx["r_V"], mx["r_qT"], mx["r_mix"], mx["r_gx"]
            gxs, pus, KsT, Vs, PTp, PTc = mx["gxs"], mx["pus"], mx["KsT"], mx["Vs"], mx["PTp"], mx["PTc"]
            r_xn = mx.pop("pre", None)
            if r_xn is None:
                r_xn = prenorm(l, 2, gi)
            A_obuf.recarve()
            pu_g = obuf[:, 0:1056].rearrange("p (c n) -> p c n", c=2)
            r_pu = A_obuf.reg("pu")
            cacc = obuf[:, 1056:2080].rearrange("p (c n) -> p c n", c=2)
            r_cacc = [A_obuf.reg("cacc0"), A_obuf.reg("cacc1")]
            sA = obuf[:, 2080:2608]
            sB = obuf[:, 2608:3136]
            r_sA, r_sB = A_obuf.reg("sA"), A_obuf.reg("sB")
            g32 = obuf[:, 3136:3392].rearrange("p (c n) -> p c n", c=2)
            r_g32 = A_obuf.reg("g32")
            p32 = obuf[:, 3392:3648].rearrange("p (c n) -> p c n", c=2)
            r_p32 = A_obuf.reg("p32")
            if not sample:
                op("dve", lambda e: e.tensor_copy(out=pu_g[:, :, 0:16], in_=puc[:, :, :]), [r_puc], [r_pu])
                if gi > 0:
                    op("dve", lambda e: e.tensor_copy(out=gx[:, :, 0:30], in_=gx[:, :, 512:542]), [r_gx], [r_gx])
            xn = lambda k: xnT[:, k, 0:n]
            for pi in range(2):
                w, rw = s4_next()
                for ci in range(2):
                    c = 2 * pi + ci
                    pb, pr = palloc()
                    mm8(pb[:, 0:n], pr, w, rw, ci * P, P, xn, [r_xn])
                    op("act", lambda e, pb=pb, c=c: e.activation(out=qT[:, c, 0:n], in_=pb[:, 0:n], func=AF.Copy, scale=0.125),
                       [pr], [r_qT])
                s4_issue()
            if ML < 2:
                for _ in range(8):
                    s4_next(); s4_issue()
                return
            w, rw = s4_next()
            pb, pr = palloc()
            mm8(pb[:, 0:n], pr, w, rw, 0, P, xn, [r_xn])
            op("dve", lambda e, pb=pb: e.tensor_copy(out=kT[:, 128 + g0:128 + g0 + n], in_=pb[:, 0:n]), [pr], [r_kT])
            for tl in range(n // P):
                tile = g0 // P + tl
                full = sample or (tile == 15)
                pv, prv = palloc()
                c0 = 0 if full else P
                for k in range(KC):
                    op("pe", lambda e, k=k, pv=pv, tl=tl, c0=c0, w=w: e.matmul(
                        pv[:, c0:256], lhsT=xnT[:, k, tl * P:(tl + 1) * P], rhs=w[:, k, c0:256],
                        start=(k == 0), stop=(k == KC - 1)), [rw, r_xn], [prv], mark=(k == KC - 1))
                op("act", lambda e, pv=pv, tile=tile: e.activation(out=Vtm[:, 1 + tile, :], in_=pv[:, P:256], func=AF.Copy),
                   [prv], [r_V])
                if full:
                    op("act", lambda e, pv=pv: e.activation(out=stgo[:, 0:256], in_=pv[:, 0:256], func=AF.Copy), [prv], [r_stgo[0]])
                    if sample:
                        for s_ in range(16):
                            out_rows(0, ks_d[l, s_, 120:128, :], stgo[8 * s_:8 * s_ + 8, 0:128])
                            out_rows(0, vs_d[l, s_, 120:128, :], stgo[8 * s_:8 * s_ + 8, 128:256])
                    else:
                        out_rows(0, kp_d[l], stgo[:, 0:128])
                        out_rows(0, vp_d[l], stgo[:, 128:256])
            s4_issue()
            if ML < 3:
                for _ in range(7):
                    s4_next(); s4_issue()
                return
            for cc in range(2):
                w, rw = s4_next()
                pa, pra = palloc()
                pg, prg = palloc()
                mm8(pa[:, 0:n], pra, w, rw, 0, P, xn, [r_xn])
                mm8(pg[:, 0:n], prg, w, rw, P, P, xn, [r_xn])
                sgt, sgr = rot("sg", sgb, r_sg)
                op("act", lambda e, pg=pg, sgt=sgt: e.activation(out=sgt[:, 0:n], in_=pg[:, 0:n], func=AF.Sigmoid),
                   [prg], [sgr])
                if sample:
                    op("dve", lambda e, pa=pa, sgt=sgt, cc=cc: e.tensor_tensor(
                        out=g32[:, cc, :], in0=sgt[:, 0:P], in1=pa[:, 0:P], op=ALU.mult), [pra, sgr], [r_g32])
                    op("dve", lambda e, cc=cc: e.tensor_copy(
                        out=gxs[:, cc, :, 30:38], in_=g32[:, cc, :].rearrange("p (s t) -> p s t", s=16)),
                       [r_g32], [mx["r_gxs"]])
                else:
                    op("dve", lambda e, pa=pa, sgt=sgt, cc=cc: e.tensor_tensor(
                        out=gx[:, cc, 30:30 + n], in0=sgt[:, 0:n], in1=pa[:, 0:n], op=ALU.mult), [pra, sgr], [r_gx])
                    if last_prompt:
                        op("dve", lambda e, pa=pa, sgt=sgt, cc=cc: e.tensor_tensor(
                            out=g32[:, cc, :], in0=sgt[:, n - P:n], in1=pa[:, n - P:n], op=ALU.mult), [pra, sgr], [r_g32])
                s4_issue()
            w, rw = s4_next()
            for pc in range(2):
                pb, pr = palloc()
                mm8(pb[:, 0:n], pr, w, rw, pc * P, P, xn, [r_xn])
                if sample:
                    op("act", lambda e, pb=pb, pc=pc: e.activation(out=p32[:, pc, :], in_=pb[:, 0:P], func=AF.Copy),
                       [pr], [r_p32])
                    op("dve", lambda e, pc=pc: e.tensor_copy(
                        out=pus[:, pc, :, 16:24], in_=p32[:, pc, :].rearrange("p (s t) -> p s t", s=16)),
                       [r_p32], [mx["r_pus"]])
                else:
                    op("act", lambda e, pb=pb, pc=pc: e.activation(out=pu_g[:, pc, 16:16 + n], in_=pb[:, 0:n], func=AF.Copy),
                       [pr], [r_pu])
            s4_issue()

            if ML < 4:
                for _ in range(4):
                    s4_next(); s4_issue()
                return
            A_xn.recarve()
            cbf = xnT[:, 0:2, :]
            csq = xnT[:, 2:4, :]
            cs_ = xnT[:, 4:6, :]
            dbfs = [xnT[:, 6, :], xnT[:, 7, :]]
            r_cbf, r_csq, r_cs = A_xn.reg("cbf"), A_xn.reg("csq"), A_xn.reg("cs")
            r_dbfs = [A_xn.reg("dbf0"), A_xn.reg("dbf1")]

            def gsrc(cc, j):
                if sample:
                    return gxs[:, cc, :, j:j + 8]
                return gx[:, cc, j:j + n]

            def cdst(cc):
                if sample:
                    return cacc[:, cc, 0:P].rearrange("p (s t) -> p s t", s=16)
                return cacc[:, cc, 0:n]

            r_g = mx["r_gxs"] if sample else r_gx
            for cc in range(2):
                pc_, prc_ = palloc()
                for j in range(31):
                    dt_, dr_ = rot("dg", dring, r_dring)
                    wcol = dwT[:, cc, l * 31 + j:l * 31 + j + 1]
                    op("act", lambda e, dt_=dt_, wcol=wcol: e.activation(out=dt_[:, :], in_=identb[:, :], func=AF.Copy, scale=wcol),
                       [r_const], [dr_])
                    rhs_ = gsrc(cc, j)
                    op("pe", lambda e, pc_=pc_, dt_=dt_, rhs_=rhs_, j=j: e.matmul(pc_[:, 0:n], lhsT=dt_[:, :], rhs=rhs_,
                                                                                  start=(j == 0), stop=(j == 30)),
                       [dr_, r_g], [prc_], mark=True)
                bcol = vT[:, cc, l * 4 + 0:l * 4 + 1]
                op("act", lambda e, pc_=pc_, cc=cc, bcol=bcol: e.activation(out=cacc[:, cc, 0:n], in_=pc_[:, 0:n], func=AF.Identity, bias=bcol),
                   [prc_, r_const], [r_cacc[cc]])
            def normalize(pO, prO, c, qcols):
                Rt, Rr = rot("R", Rb, r_R)
                col = l * 4 + c
                op("dve", lambda e: e.tensor_scalar(out=Rt[:, :], in0=pO[:, P:2 * P], scalar1=esc[:, col:col + 1], scalar2=None,
                                                    op0=ALU.add), [prO, r_const], [Rr])
                op("dve", lambda e: e.reciprocal(out=Rt[:, :], in_=Rt[:, :]), [Rr], [Rr])
                op("dve", lambda e: e.tensor_tensor(out=mixT[:, c, qcols], in0=pO[:, 0:P], in1=Rt[:, :], op=ALU.mult),
                   [prO, Rr], [r_mix[c]])

            norm_pending = []

            def normalize2(items, qcols):
                Rs = [rot("R", Rb, r_R) for _ in items]
                for (pO, prO, c), (Rt, Rr) in zip(items, Rs):
                    col = l * 4 + c
                    op("dve", lambda e, Rt=Rt, pO=pO, col=col: e.tensor_scalar(out=Rt[:, :], in0=pO[:, P:2 * P], scalar1=esc[:, col:col + 1],
                                                                              scalar2=None, op0=ALU.add), [prO, r_const], [Rr])
                for (pO, prO, c), (Rt, Rr) in zip(items, Rs):
                    op("dve", lambda e, Rt=Rt: e.reciprocal(out=Rt[:, :], in_=Rt[:, :]), [Rr], [Rr])
                for (pO, prO, c), (Rt, Rr) in zip(items, Rs):
                    op("dve", lambda e, Rt=Rt, pO=pO, c=c: e.tensor_tensor(out=mixT[:, c, qcols], in0=pO[:, 0:P], in1=Rt[:, :], op=ALU.mult),
                       [prO, Rr], [r_mix[c]])

            def att_qblock(j):
                qb = g0 // P + j
                for cp in range(2):
                    pSs = [palloc(), palloc()]
                    for hi in range(2):
                        pS, prS = pSs[hi]
                        t0 = (cp * 2 + hi) * 512
                        if qb == 0:
                            for seg in range(4):
                                rhs_ = negt[:, :] if seg % 2 == 0 else maskp[:, t0 + seg * P:t0 + (seg + 1) * P]
                                op("pe", lambda e, pS=pS, seg=seg, rhs_=rhs_: e.matmul(
                                    pS[:, seg * P:(seg + 1) * P], lhsT=identb[:, :], rhs=rhs_,
                                    start=(seg == 0), stop=False, skip_group_check=True), [r_const], [prS], mark=False)
                        else:
                            op("pe", lambda e, pS=pS, t0=t0: e.matmul(pS[:, :], lhsT=identb[:, :], rhs=maskp[:, t0:t0 + 512],
                                                                      start=True, stop=False), [r_const], [prS], mark=False)
                    for c2 in range(2):
                        c = cp * 2 + c2
                        for kb in range(2):
                            kc0 = qb * P + kb * P
                            last = (c2 == 1 and kb == 1)
                            for hi in range(2):
                                pS, prS = pSs[hi]
                                seg = c2 * 2 + kb
                                sgc = (qb == 0)
                                op("pe", lambda e, pS=pS, c=c, hi=hi, seg=seg, kc0=kc0, j=j, last=last, sgc=sgc: e.matmul(
                                    pS[:, seg * P:(seg + 1) * P],
                                    lhsT=kT[hi * 64:(hi + 1) * 64, kc0:kc0 + P], rhs=qT[hi * 64:(hi + 1) * 64, c, j * P:(j + 1) * P],
                                    start=False, stop=last, skip_group_check=sgc), [r_kT, r_qT], [prS], mark=last)
                    PTs = []
                    for hi in range(2):
                        pS, prS = pSs[hi]
                        PT, rPT = rot("PT", PTb, r_PT)
                        op("act", lambda e, pS=pS, PT=PT: e.activation(out=PT[:, :], in_=pS[:, :], func=AF.Exp), [prS], [rPT])
                        PTs.append((PT, rPT))
                    for c2 in range(2):
                        c = cp * 2 + c2
                        pO, prO = palloc()
                        for hi in range(2):
                            PT, rPT = PTs[hi]
                            for which_ in range(2):
                                for kb in range(2):
                                    last = (hi == 1 and which_ == 1 and kb == 1)
                                    seg = c2 * 2 + kb
                                    lhs = Vtm[:, qb + kb, hi * 64:(hi + 1) * 64] if which_ == 0 else onesb[:, 0:64]
                                    op("pe", lambda e, pO=pO, hi=hi, which_=which_, kb=kb, lhs=lhs, PT=PT, seg=seg: e.matmul(
                                        pO[hi * 64:(hi + 1) * 64, which_ * P:(which_ + 1) * P], lhsT=lhs,
                                        rhs=PT[:, seg * P:(seg + 1) * P],
                                        start=(kb == 0), stop=(kb == 1)), [r_V, rPT, r_const], [prO], mark=last)
                        norm_pending.append((pO, prO, c))
                    normalize2(norm_pending, slice(j * P, (j + 1) * P))
                    del norm_pending[:]
                if j == 2 and gi + 1 < NGm:
                    mx["pre_st"] = prenorm_stats_m(l, 2, gi + 1)

            def att_sample():
                pSp = [palloc(), palloc()]
                for kvh in range(2):
                    pS, prS = pSp[kvh]
                    op("pe", lambda e, pS=pS, kvh=kvh: e.matmul(pS[:, :], lhsT=identb[:, :], rhs=msp[:, kvh * 512:(kvh + 1) * 512],
                                                                start=True, stop=False), [r_const], [prS], mark=False)
                for s_ in range(16):
                    for kvh in range(2):
                        pS, prS = pSp[kvh]
                        last = (s_ == 15)
                        op("pe", lambda e, pS=pS, s_=s_, kvh=kvh, last=last: e.matmul(
                            pS[:, s_ * 32:(s_ + 1) * 32],
                            lhsT=KsT[kvh * 64:(kvh + 1) * 64, s_, :], rhs=qT[kvh * 64:(kvh + 1) * 64, :, s_ * 8:(s_ + 1) * 8],
                            start=False, stop=last), [mx["r_KsT"], r_qT], [prS], mark=last)
                for kvh in range(2):
                    pS, prS = pSp[kvh]
                    op("act", lambda e, pS=pS, kvh=kvh: e.activation(out=PTp[:, kvh * 512:(kvh + 1) * 512], in_=pS[:, :], func=AF.Exp),
                       [prS], [mx["r_PTp"]])
                KnT = kT[:, 128 + g0:128 + g0 + P]
                pSc = [palloc(), palloc()]
                for hi in range(2):
                    pS, prS = pSc[hi]
                    op("pe", lambda e, pS=pS, hi=hi: e.matmul(pS[:, :], lhsT=identb[:, :], rhs=msc[:, hi * 512:(hi + 1) * 512],
                                                              start=True, stop=False), [r_const], [prS], mark=False)
                for c in range(4):
                    for hi in range(2):
                        pS, prS = pSc[hi]
                        last = (c == 3)
                        op("pe", lambda e, pS=pS, c=c, hi=hi, last=last: e.matmul(
                            pS[:, c * P:(c + 1) * P],
                            lhsT=KnT[hi * 64:(hi + 1) * 64, :], rhs=qT[hi * 64:(hi + 1) * 64, c, 0:P],
                            start=False, stop=last), [r_kT, r_qT], [prS], mark=last)
                for hi in range(2):
                    pS, prS = pSc[hi]
                    op("act", lambda e, pS=pS, hi=hi: e.activation(out=PTc[:, hi * 512:(hi + 1) * 512], in_=pS[:, :], func=AF.Exp),
                       [prS], [mx["r_PTc"]])
                pOs, prOs = palloc()
                pSm, prSm = palloc()
                for (pb_, pr_, isO) in ((pOs, prOs, True), (pSm, prSm, False)):
                    pb4 = pb_[:, :].rearrange("p (g q) -> p g q", g=4)
                    pb5 = pb_[:, :].rearrange("p (g s t) -> p g s t", g=4, s=16)
                    for hi in range(2):
                        for g in range(4):
                            lhs = Vtm[:, 17, hi * 64:(hi + 1) * 64] if isO else onesb[:, 0:64]
                            op("pe", lambda e, pb4=pb4, hi=hi, g=g, lhs=lhs: e.matmul(
                                pb4[hi * 64:(hi + 1) * 64, g, :], lhsT=lhs, rhs=PTc[:, hi * 512 + g * P:hi * 512 + (g + 1) * P],
                                start=(g == 0), stop=False, skip_group_check=True),
                               [r_V, mx["r_PTc"], r_const], [pr_], mark=False)
                        for s_ in range(16):
                            lhs = Vs[:, s_, hi * 64:(hi + 1) * 64] if isO else onesb[:, 0:64]
                            for g in range(4):
                                last = (hi == 1 and s_ == 15 and g == 3)
                                c0 = hi * 512 + s_ * 32 + g * 8
                                op("pe", lambda e, pb5=pb5, hi=hi, s_=s_, lhs=lhs, g=g, c0=c0: e.matmul(
                                    pb5[hi * 64:(hi + 1) * 64, g, s_, :], lhsT=lhs, rhs=PTp[:, c0:c0 + 8],
                                    start=False, stop=(s_ == 15), skip_group_check=True),
                                   [mx["r_Vs"], mx["r_PTp"], r_const], [pr_], mark=last)
                for c in range(4):
                    Rt, Rr = rot("R", Rb, r_R)
                    col = l * 4 + c
                    op("dve", lambda e, Rt=Rt, c=c, col=col: e.tensor_scalar(
                        out=Rt[:, :], in0=pSm[:, c * P:(c + 1) * P], scalar1=esc[:, col:col + 1], scalar2=None,
                        op0=ALU.add), [prSm, r_const], [Rr])
                    op("dve", lambda e, Rt=Rt: e.reciprocal(out=Rt[:, :], in_=Rt[:, :]), [Rr], [Rr])
                    op("dve", lambda e, Rt=Rt, c=c: e.tensor_tensor(out=mixT[:, c, 0:P], in0=pOs[:, c * P:(c + 1) * P], in1=Rt[:, :],
                                                                   op=ALU.mult), [prOs, Rr], [r_mix[c]])


            def ln_part():
                for cc in range(2):
                    op("act", lambda e, cc=cc: e.activation(out=cbf[:, cc, 0:n], in_=cacc[:, cc, 0:n], func=AF.Copy),
                       [r_cacc[cc]], [r_cbf])
                    op("act", lambda e, cc=cc: e.activation(out=csq[:, cc, 0:n], in_=cacc[:, cc, 0:n], func=AF.Square),
                       [r_cacc[cc]], [r_csq])
                pm, prm = palloc()
                pq, prq = palloc()
                for cc in range(2):
                    op("pe", lambda e, cc=cc: e.matmul(pm[:, 0:n], lhsT=ones256[:, :], rhs=cbf[:, cc, 0:n], start=(cc == 0), stop=(cc == 1)),
                       [r_cbf, r_const], [prm], mark=(cc == 1))
                for cc in range(2):
                    op("pe", lambda e, cc=cc: e.matmul(pq[:, 0:n], lhsT=ones256[:, :], rhs=csq[:, cc, 0:n], start=(cc == 0), stop=(cc == 1)),
                       [r_csq, r_const], [prq], mark=(cc == 1))
                mt, mr = rot("tmp", tmpb, r_tmp)
                vt, vr = rot("tmp", tmpb, r_tmp)
                op("act", lambda e: e.activation(out=mt[:, 0:n], in_=pm[:, 0:n], func=AF.Copy), [prm], [mr])
                op("dve", lambda e: e.tensor_tensor(out=vt[:, 0:n], in0=mt[:, 0:n], in1=mt[:, 0:n], op=ALU.mult), [mr], [vr])
                op("dve", lambda e: e.tensor_tensor(out=vt[:, 0:n], in0=pq[:, 0:n], in1=vt[:, 0:n], op=ALU.subtract), [prq, vr], [vr])
                op("act", lambda e: e.activation(out=vt[:, 0:n], in_=vt[:, 0:n], func=AF.Sqrt, bias=epsl[:, 0:1]), [vr, r_const], [vr])
                op("dve", lambda e: e.reciprocal(out=vt[:, 0:n], in_=vt[:, 0:n]), [vr], [vr])
                for cc in range(2):
                    op("dve", lambda e, cc=cc: e.tensor_tensor(out=cacc[:, cc, 0:n], in0=cacc[:, cc, 0:n], in1=mt[:, 0:n], op=ALU.subtract),
                       [r_cacc[cc], mr], [r_cacc[cc]])
                    op("dve", lambda e, cc=cc: e.tensor_tensor(out=cacc[:, cc, 0:n], in0=cacc[:, cc, 0:n], in1=vt[:, 0:n], op=ALU.mult),
                       [r_cacc[cc], vr], [r_cacc[cc]])
                    op("act", lambda e, cc=cc: e.activation(out=cs_[:, cc, 0:n], in_=cacc[:, cc, 0:n], func=AF.Silu,
                                                            scale=vT[:, cc, l * 4 + 1:l * 4 + 2], bias=vT[:, cc, l * 4 + 2:l * 4 + 3]),
                       [r_cacc[cc], r_const], [r_cs])

            def pw_part():
                for oc in range(2):
                    pb, pr = palloc()
                    for kc in range(2):
                        op("pe", lambda e, pb=pb, oc=oc, kc=kc: e.matmul(pb[:, 0:n], lhsT=pwb[:, l, kc, oc * P:(oc + 1) * P],
                                                                          rhs=cs_[:, kc, 0:n], start=(kc == 0), stop=(kc == 1)),
                           [r_cs, r_const2], [pr], mark=(kc == 1))
                    op("act", lambda e, pb=pb, oc=oc: e.activation(out=mixT[:, 4 + oc, 0:n], in_=pb[:, 0:n], func=AF.Copy),
                       [pr], [r_mix[4 + oc]])
                if sample or last_prompt:
                    pb, pr = palloc()
                    for cc in range(2):
                        op("pe", lambda e, pb=pb, cc=cc: e.transpose(out=pb[:, cc * P:(cc + 1) * P], in_=g32[:, cc, :], identity=ident[:, :]),
                           [r_g32, r_const], [pr], mark=(cc == 1))
                    op("dve", lambda e, pb=pb: e.tensor_copy(out=stgo[:, 256:512], in_=pb[:, 0:256]), [pr], [r_stgo[1]])
                    if sample:
                        for s_ in range(16):
                            out_rows(1, cs_d[l, s_, 22:30, :], stgo[8 * s_:8 * s_ + 8, 256:512])
                    else:
                        out_rows(1, cp_d[l], stgo[98:128, 256:512])


            def pool_dve(pc):
                if sample:
                    pv_ = pus[:, pc, :, :]
                    sAv = sA[:, 0:384].rearrange("p (s t) -> p s t", s=16)
                    sBv = sB[:, 0:384].rearrange("p (s t) -> p s t", s=16)
                    W_ = 24
                    sl = lambda v, a, b: v[:, :, a:b]
                    r_p = mx["r_pus"]
                    dv = lambda lo, hi_: dbfs[pc][lo:hi_, 0:P].rearrange("p (s t) -> p s t", s=16)
                    hsl = lambda v, lo, hi_, a, b: v[lo:hi_, :, a:b]
                else:
                    pv_ = pu_g[:, pc, :]
                    sAv, sBv = sA, sB
                    W_ = 16 + n
                    sl = lambda v, a, b: v[:, a:b]
                    r_p = r_pu
                    dv = lambda lo, hi_: dbfs[pc][lo:hi_, 0:n]
                    hsl = lambda v, lo, hi_, a, b: v[lo:hi_, a:b]

                def tt(out_, in0_, in1_, rd, wr, opx=ALU.add):
                    op("dve", lambda e, out_=out_, in0_=in0_, in1_=in1_, opx=opx: e.tensor_tensor(out=out_, in0=in0_, in1=in1_, op=opx),
                       rd, wr)

                tt(sl(sAv, 1, W_), sl(pv_, 1, W_), sl(pv_, 0, W_ - 1), [r_p], [r_sA])
                tt(sl(sBv, 3, W_), sl(sAv, 3, W_), sl(sAv, 1, W_ - 2), [r_sA], [r_sB])
                if pc == 1:
                    tt(sl(sAv, 7, W_), sl(sBv, 7, W_), sl(sBv, 3, W_ - 4), [r_sB], [r_sA])
                    tt(sl(sBv, 15, W_), sl(sAv, 15, W_), sl(sAv, 7, W_ - 8), [r_sA], [r_sB])
                for (lo, hi_, sv_, rs_) in ((0, 64, sAv, r_sA), (64, 128, sBv, r_sB)):
                    o_, i0_, i1_, sc_ = dv(lo, hi_), hsl(sv_, lo, hi_, 16, W_), hsl(pv_, lo, hi_, 16, W_), invw[lo:hi_, pc:pc + 1]
                    op("dve", lambda e, o_=o_, i0_=i0_, i1_=i1_, sc_=sc_: e.scalar_tensor_tensor(
                        out=o_, in0=i0_, scalar=sc_, in1=i1_, op0=ALU.mult, op1=ALU.subtract), [rs_, r_p, r_const], [r_dbfs[pc]])
                    if gi == 0:
                        tb, tr = rot("tmp", tmpb, r_tmp)
                        tt(tb[lo:hi_, 0:16], sv_[lo:hi_, 16:32], invcnt[lo:hi_, pc * 16:(pc + 1) * 16], [rs_, r_const], [tr], ALU.mult)
                        tt(dbfs[pc][lo:hi_, 0:16], tb[lo:hi_, 0:16], pu_g[lo:hi_, pc, 16:32], [tr, r_p], [r_dbfs[pc]], ALU.subtract)

            def pool_mm(pc):
                pb, pr = palloc()
                op("pe", lambda e, pb=pb, pc=pc: e.matmul(pb[:, 0:n], lhsT=poolwb[:, l, pc, :], rhs=dbfs[pc][:, 0:n], start=True, stop=True),
                   [r_dbfs[pc], r_const2], [pr])
                op("act", lambda e, pb=pb, pc=pc: e.activation(out=mixT[:, 6 + pc, 0:n], in_=pb[:, 0:n], func=AF.Copy,
                                                               scale=vT[:, pc, l * 4 + 3:l * 4 + 4]), [pr, r_const], [r_mix[6 + pc]])

            def pool_tail():
                if not sample:
                    op("dve", lambda e: e.tensor_copy(out=puc[:, :, :], in_=pu_g[:, :, n:n + 16]), [r_pu], [r_puc])
                if sample or last_prompt:
                    pb, pr = palloc()
                    for pc in range(2):
                        src = p32[:, pc, :] if sample else pu_g[:, pc, 16 + n - P:16 + n]
                        rsrc = r_p32 if sample else r_pu
                        op("pe", lambda e, pb=pb, pc=pc, src=src: e.transpose(out=pb[:, pc * P:(pc + 1) * P], in_=src, identity=ident[:, :]),
                           [rsrc, r_const], [pr], mark=(pc == 1))
                    op("dve", lambda e, pb=pb: e.tensor_copy(out=stgo[:, 512:768], in_=pb[:, 0:256]), [pr], [r_stgo[2]])
                    if sample:
                        for s_ in range(16):
                            out_rows(2, ps_d[l, s_, 7:15, :], stgo[8 * s_:8 * s_ + 8, 512:768])
                    else:
                        out_rows(2, pp_d[l], stgo[113:128, 512:768])


            if not sample:
                att_qblock(0)
                ln_part()
                att_qblock(1)
                pool_dve(0)
                pool_dve(1)
                att_qblock(2)
                att_qblock(3)
                pw_part()
                pool_mm(0)
                pool_mm(1)
                pool_tail()
            else:
                att_sample()
                ln_part()
                pw_part()
                pool_dve(0)
                pool_dve(1)
                pool_mm(0)
                pool_mm(1)
                pool_tail()

            if ML < 7:
                for _ in range(4):
                    s4_next(); s4_issue()
                return
            if gi + 1 < NGm and ML >= 7:
                if "pre_st" in mx:
                    mx["pre"] = prenorm_apply_m(l, 2, gi + 1, *mx.pop("pre_st"))
                else:
                    mx["pre"] = prenorm(l, 2, gi + 1)
            A_obuf.recarve()
            r_ob = A_obuf.reg("ob")
            for pi in range(4):
                w, rw = s4_next()
                for ci in range(2):
                    d = 2 * pi + ci
                    pb, pr = palloc()
                    for k in range(KC):
                        op("pe", lambda e, pb=pb, k=k, ci=ci, w=w: e.matmul(
                            pb[:, 0:n], lhsT=w[:, k, ci * P:(ci + 1) * P], rhs=mixT[:, k, 0:n],
                            start=(k == 0), stop=(k == KC - 1)), [rw, r_mix[k]], [pr], mark=(k == KC - 1))
                    proj_out_evac(pb, pr, d, n, r_ob)
                s4_issue()
            postnorm(l, 3, gi, r_ob, half=False)

        pending_post = []

        def flush_post():
            while pending_post:
                pending_post.pop(0)()

        def ffn_phase(l, which, carry):
            ipre = 0 if which == 1 else 4
            r_xn = carry if carry is not None else f_prenorm(l, ipre, 0)
            for gi in range(NFG):
                if gi + 1 < NFG:
                    nxt = (l, ipre, gi + 1)
                elif which == 2 and l + 1 < L and 'f1' in PH:
                    nxt = (l + 1, 0, 0)
                else:
                    nxt = None
                r_xn = ffn(l, which, gi, r_xn, nxt)
            return r_xn

        carry = None
        for l in range(L):
            if 'f1' in PH:
                carry = ffn_phase(l, 1, carry)
                flush_post()
            if 'mx' in PH:
                mixer_begin(l)
                for gi in range(NG):
                    mixer(l, gi)
            if 'f1' in PH and 'mx' in PH:
                for _ in range(4):
                    s11_issue()
            if 'f2' in PH:
                carry = ffn_phase(l, 2, None)
                if carry is None:
                    flush_post()
        flush_post()

        A_big.recarve()
        yst = [big[:, i * 2048:(i + 1) * 2048].bitcast(F32) for i in range(2)]
        r_yst = [A_big.reg("yst0"), A_big.reg("yst1")]
        for t in range(NT // P):
            i = t % 2
            gi = min(t // 4, 4)
            for hb in range(2):
                pb, pr = palloc()
                for kk in range(4):
                    k = hb * 4 + kk
                    op("pe", lambda e, pb=pb, kk=kk, k=k, t=t: e.transpose(out=pb[:, kk * P:(kk + 1) * P],
                                                                         in_=xT[:, k, t * P:(t + 1) * P], identity=ident[:, :]),
                       xregs(gi) + [r_const], [pr], mark=(kk == 3))
                if hb == 0:
                    op("act", lambda e, pb=pb, i=i: e.activation(out=yst[i][:, 0:512], in_=pb[:, :], func=AF.Copy), [pr], [r_yst[i]])
                else:
                    op("dve", lambda e, pb=pb, i=i: e.tensor_copy(out=yst[i][:, 512:1024], in_=pb[:, :]), [pr], [r_yst[i]])
            dma("sp", y_d[t * P:(t + 1) * P, :], yst[i], [r_yst[i]], [], d_y[i])

        for dsem in d_stgo + d_y + [d_cp]:
            if dsem.count:
                T.wait_tok("sp", (dsem, dsem.count))

        block = es.enter_context(nc.Block())
        T.replay(block)
    return nc


_CACHE = {}


def _perm_q():
    idx = []
    for c in range(4):
        for h in (c, c + 4):
            idx.extend(range(h * 64, (h + 1) * 64))
    return np.array(idx)


def prep_inputs(inp, DEPTH):
    L = DEPTH
    f32 = lambda a: np.ascontiguousarray(np.asarray(a, dtype=np.float32))
    pq = _perm_q()
    cols = list(pq) + list(range(512, 768)) + list(range(768, 896)) + list(range(1024, 1152)) + \
        list(range(896, 1024)) + list(range(1152, 1280)) + list(range(1280, 1536))
    cols = np.array(cols)
    w_in = f32(inp["w_in"])[:L][:, :, cols]
    rows = np.concatenate([pq, np.arange(512, 1024)])
    w_out = f32(inp["w_out"])[:L][:, rows, :]
    shared = {
        "ng": f32(inp["norm_g"])[:L].reshape(L * 48, 128),
        "f1g": f32(inp["ffn1_wg"])[:L], "f1u": f32(inp["ffn1_wu"])[:L], "f1d": f32(inp["ffn1_wd"])[:L],
        "win": np.ascontiguousarray(w_in), "wout": np.ascontiguousarray(w_out),
        "f2g": f32(inp["ffn2_wg"])[:L], "f2u": f32(inp["ffn2_wu"])[:L], "f2d": f32(inp["ffn2_wd"])[:L],
        "sinks": f32(inp["attn_sinks"])[:L].reshape(1, L * 8),
        "dww": f32(inp["conv_dw_w"])[:L].reshape(L * 31, 256),
        "vecs": np.ascontiguousarray(np.stack([f32(inp["conv_dw_b"])[:L], f32(inp["conv_ln_g"])[:L],
                                               f32(inp["conv_ln_b"])[:L], f32(inp["pool_scale"])[:L]], axis=1).reshape(L * 4, 256)),
        "pww": f32(inp["conv_pw_w"])[:L],
        "poolw": f32(inp["pool_w"])[:L].reshape(L, 256, 64),
    }
    shared.update(make_consts())
    xp, xs = f32(inp["x_prompt"]), f32(inp["x_sample"])
    sk, sv = f32(inp["state_attn_k"])[:L], f32(inp["state_attn_v"])[:L]
    sc, sp = f32(inp["state_conv"])[:L], f32(inp["state_pool"])[:L]
    maps = []
    for c in range(NCORES):
        m = dict(shared)
        m["x"] = np.ascontiguousarray(np.concatenate([xp[c], xs[16 * c:16 * c + 16].reshape(128, D)], axis=0))
        m["sk"] = np.ascontiguousarray(sk[:, 16 * c:16 * c + 16].reshape(L, 16, 128, 128))
        m["sv"] = np.ascontiguousarray(sv[:, 16 * c:16 * c + 16].reshape(L, 16, 128, 128))
        m["sc"] = np.ascontiguousarray(sc[:, 16 * c:16 * c + 16])
        m["sp"] = np.ascontiguousarray(sp[:, 16 * c:16 * c + 16])
        maps.append(m)
    return maps


def assemble(results, DEPTH):
    L = DEPTH
    R = results
    y = np.stack([r["y"] for r in R])
    y_prompt = np.ascontiguousarray(y[:, :NPR, :])
    y_sample = np.ascontiguousarray(y[:, NPR:, :].reshape(128, 8, D))
    kp = np.stack([r["kp"] for r in R], axis=1).reshape(L, 8, 128, 2, 64)
    vp = np.stack([r["vp"] for r in R], axis=1).reshape(L, 8, 128, 2, 64)
    cp = np.stack([r["cp"] for r in R], axis=1)
    pp = np.stack([r["pp"] for r in R], axis=1)
    ks = np.concatenate([r["ks"] for r in R], axis=1).reshape(L, 128, 128, 2, 64)
    vs = np.concatenate([r["vs"] for r in R], axis=1).reshape(L, 128, 128, 2, 64)
    cs = np.concatenate([r["cs"] for r in R], axis=1)
    ps = np.concatenate([r["ps"] for r in R], axis=1)
    outs = (y_prompt, y_sample, kp, vp, cp, pp, ks, vs, cs, ps)
    return tuple(np.ascontiguousarray(o, dtype=np.float32) for o in outs)


def run(inp, DEPTH, trace=False, **bk):
    key = (DEPTH, tuple(sorted(bk.items())))
    if key not in _CACHE:
        _CACHE[key] = build_program(DEPTH, **bk)
    nc = _CACHE[key]
    maps = prep_inputs(inp, DEPTH)
    res = run_bass_kernel_spmd(nc, maps, core_ids=list(range(NCORES)), **({"trace": True} if trace else {}))
    return assemble(res.results, DEPTH), res


def kernel(**inputs):
    outs, _ = run(inputs, 4)
    return outs
```
